# Optimizing a Trainium2 kernel written in Bass

```python
import jax, jax.numpy as jnp
from jax import lax
import numpy as np

D_MODEL = 1024
BATCH = 8
SEQ = 2048
DEPTH = 1
DEC_BATCH = 128
DEC_SEQ = 8
PAST_LEN = 16384
PAGE_SIZE = 128

D_LRU = D_MODEL
N_LRU_HEADS = 8
LRU_HEAD_DIM = D_LRU // N_LRU_HEADS
LRU_CONV_W = 4
LRU_C = 8.0
D_CONV = D_MODEL
CONF_CONV_W = 31
D_MIX = D_LRU + D_CONV
D_IN = 2 * D_LRU + 2 * D_CONV
D_FF = 2816
FFN_RES = 0.5
N_MEM = 256
N_MEM_HEADS = 4
MEM_HEAD_DIM = D_MODEL // N_MEM_HEADS
EPS = 1e-6

kernel_name = "hymba_rglru_conformer_conv_decoder_step"


def rmsnorm(x, g):
    xf = x.astype(jnp.float32)
    y = xf * lax.rsqrt(jnp.mean(xf * xf, axis=-1, keepdims=True) + EPS)
    return (y * g.astype(jnp.float32)).astype(x.dtype)


def layernorm(x, g, b):
    xf = x.astype(jnp.float32)
    xc = xf - jnp.mean(xf, axis=-1, keepdims=True)
    y = xc * lax.rsqrt(jnp.mean(xc * xc, axis=-1, keepdims=True) + EPS)
    return (y * g.astype(jnp.float32) + b.astype(jnp.float32)).astype(x.dtype)


def swiglu(x, w_gate, w_up, w_down):
    return (jax.nn.silu(x @ w_gate) * (x @ w_up)) @ w_down


def causal_dwconv(x, buf, w, b):
    xp = jnp.concatenate([buf.astype(x.dtype), x], axis=1)
    y = lax.conv_general_dilated(
        xp, w[:, None, :].astype(x.dtype), window_strides=(1,), padding='VALID',
        dimension_numbers=('NWC', 'WIO', 'NWC'), feature_group_count=x.shape[-1])
    new_buf = xp[:, xp.shape[1] - (w.shape[0] - 1):, :]
    return y + b.astype(x.dtype), new_buf


def rg_lru(x, h0, w_a, b_a, w_x, b_x, lam):
    bsz, t_len, _ = x.shape
    xh = x.reshape(bsz, t_len, N_LRU_HEADS, LRU_HEAD_DIM)
    r = jax.nn.sigmoid(jnp.einsum('bthi,hij->bthj', xh, w_a) + b_a.reshape(N_LRU_HEADS, LRU_HEAD_DIM))
    i = jax.nn.sigmoid(jnp.einsum('bthi,hij->bthj', xh, w_x) + b_x.reshape(N_LRU_HEADS, LRU_HEAD_DIM))
    r = r.reshape(bsz, t_len, D_LRU).astype(jnp.float32)
    i = i.reshape(bsz, t_len, D_LRU).astype(jnp.float32)
    log_a = -LRU_C * r * jax.nn.softplus(-lam.astype(jnp.float32))
    a = jnp.exp(log_a)
    u = jnp.sqrt(-jnp.expm1(2.0 * log_a)) * (i * x.astype(jnp.float32))

    def combine(left, right):
        a_l, b_l = left
        a_r, b_r = right
        return a_l * a_r, a_r * b_l + b_r

    a_cum, b_cum = lax.associative_scan(combine, (a, u), axis=1)
    h = a_cum * h0.astype(jnp.float32)[:, None, :] + b_cum
    return h.astype(x.dtype), h[:, -1].astype(x.dtype)


def token_mixer(h, conv4_buf, lru_h0, conv31_buf, p):
    z = h @ p['w_in']
    z_x, z_g, z_v, z_gate = jnp.split(z, [D_LRU, 2 * D_LRU, 2 * D_LRU + D_CONV], axis=-1)
    xc, new_conv4 = causal_dwconv(z_x, conv4_buf, p['lru_conv_w'], p['lru_conv_b'])
    lru, new_h = rg_lru(xc, lru_h0, p['lru_w_a'], p['lru_b_a'], p['lru_w_x'], p['lru_b_x'], p['lru_lambda'])
    y_lru = jax.nn.gelu(z_g) * lru
    glu = z_v * jax.nn.sigmoid(z_gate)
    c, new_conv31 = causal_dwconv(glu, conv31_buf, p['conf_conv_w'], p['conf_conv_b'])
    y_conf = jax.nn.silu(layernorm(c, p['conf_ln_g'], p['conf_ln_b']))
    out = jnp.concatenate([y_lru, y_conf], axis=-1) @ p['w_out']
    return out, new_conv4, new_h, new_conv31


def memory_kv(mem, g, w_k, w_v):
    bsz = mem.shape[0]
    m = rmsnorm(mem, g)
    k = (m @ w_k).reshape(bsz, N_MEM, N_MEM_HEADS, MEM_HEAD_DIM)
    v = (m @ w_v).reshape(bsz, N_MEM, N_MEM_HEADS, MEM_HEAD_DIM)
    return k, v


def memory_attention(h, k, v, w_q, w_o):
    bsz, t_len, _ = h.shape
    q = (h @ w_q).reshape(bsz, t_len, N_MEM_HEADS, MEM_HEAD_DIM).astype(jnp.float32)
    s = jnp.einsum('bthd,bmhd->bhtm', q, k.astype(jnp.float32)) * (MEM_HEAD_DIM ** -0.5)
    prob = jax.nn.softmax(s, axis=-1)
    o = jnp.einsum('bhtm,bmhd->bthd', prob, v.astype(jnp.float32))
    return o.reshape(bsz, t_len, D_MODEL).astype(h.dtype) @ w_o


def decoder_layer(x, conv4_buf, lru_h0, conv31_buf, mem_k, mem_v, p):
    x = x + FFN_RES * rmsnorm(swiglu(rmsnorm(x, p['g_ff1_pre']), p['ff1_w_gate'], p['ff1_w_up'], p['ff1_w_down']), p['g_ff1_post'])
    mix, new_conv4, new_h, new_conv31 = token_mixer(rmsnorm(x, p['g_mix_pre']), conv4_buf, lru_h0, conv31_buf, p)
    x = x + rmsnorm(mix, p['g_mix_post'])
    x = x + rmsnorm(memory_attention(rmsnorm(x, p['g_mem_pre']), mem_k, mem_v, p['w_q'], p['w_o']), p['g_mem_post'])
    x = x + FFN_RES * rmsnorm(swiglu(rmsnorm(x, p['g_ff2_pre']), p['ff2_w_gate'], p['ff2_w_up'], p['ff2_w_down']), p['g_ff2_post'])
    return x, new_conv4, new_h, new_conv31


def setup_inputs(seed: int = 0) -> dict:
    key = jax.random.key(seed)
    ks = iter(jax.random.split(key, 48))

    def nrm(shape, scale):
        return jax.random.normal(next(ks), shape, jnp.float32) * scale

    def gain():
        return 1.0 + nrm((DEPTH, D_MODEL), 0.02)

    d = D_MODEL
    inputs = {}
    inputs['x_prompt'] = nrm((BATCH, SEQ, d), 1.0)
    inputs['x_sample'] = nrm((DEC_BATCH, DEC_SEQ, d), 1.0)
    inputs['mem_prompt'] = nrm((BATCH, N_MEM, d), 1.0)
    inputs['state_lru_conv'] = nrm((DEPTH, DEC_BATCH, LRU_CONV_W - 1, D_LRU), 1.0)
    inputs['state_lru_h'] = nrm((DEPTH, DEC_BATCH, D_LRU), 0.5)
    inputs['state_conf_conv'] = nrm((DEPTH, DEC_BATCH, CONF_CONV_W - 1, D_CONV), 0.5)
    inputs['cache_mem_k'] = nrm((DEPTH, DEC_BATCH, N_MEM, N_MEM_HEADS, MEM_HEAD_DIM), 1.0)
    inputs['cache_mem_v'] = nrm((DEPTH, DEC_BATCH, N_MEM, N_MEM_HEADS, MEM_HEAD_DIM), 1.0)
    inputs['g_ff1_pre'] = gain()
    inputs['ff1_w_gate'] = nrm((DEPTH, d, D_FF), d ** -0.5)
    inputs['ff1_w_up'] = nrm((DEPTH, d, D_FF), d ** -0.5)
    inputs['ff1_w_down'] = nrm((DEPTH, D_FF, d), D_FF ** -0.5)
    inputs['g_ff1_post'] = gain()
    inputs['g_mix_pre'] = gain()
    inputs['w_in'] = nrm((DEPTH, d, D_IN), d ** -0.5)
    inputs['lru_conv_w'] = nrm((DEPTH, LRU_CONV_W, D_LRU), LRU_CONV_W ** -0.5)
    inputs['lru_conv_b'] = nrm((DEPTH, D_LRU), 0.01)
    inputs['lru_w_a'] = nrm((DEPTH, N_LRU_HEADS, LRU_HEAD_DIM, LRU_HEAD_DIM), LRU_HEAD_DIM ** -0.5)
    inputs['lru_b_a'] = nrm((DEPTH, D_LRU), 0.01)
    inputs['lru_w_x'] = nrm((DEPTH, N_LRU_HEADS, LRU_HEAD_DIM, LRU_HEAD_DIM), LRU_HEAD_DIM ** -0.5)
    inputs['lru_b_x'] = nrm((DEPTH, D_LRU), 0.01)
    a_c = jax.random.uniform(next(ks), (DEPTH, D_LRU), jnp.float32, 0.9, 0.999)
    a0 = a_c ** (1.0 / LRU_C)
    inputs['lru_lambda'] = jnp.log(a0) - jnp.log1p(-a0)
    inputs['conf_conv_w'] = nrm((DEPTH, CONF_CONV_W, D_CONV), CONF_CONV_W ** -0.5)
    inputs['conf_conv_b'] = nrm((DEPTH, D_CONV), 0.01)
    inputs['conf_ln_g'] = 1.0 + nrm((DEPTH, D_CONV), 0.02)
    inputs['conf_ln_b'] = nrm((DEPTH, D_CONV), 0.01)
    inputs['w_out'] = nrm((DEPTH, D_MIX, d), D_MIX ** -0.5)
    inputs['g_mix_post'] = gain()
    inputs['g_mem_pre'] = gain()
    inputs['g_mem_kv'] = gain()
    inputs['w_mem_k'] = nrm((DEPTH, d, d), d ** -0.5)
    inputs['w_mem_v'] = nrm((DEPTH, d, d), d ** -0.5)
    inputs['w_q'] = nrm((DEPTH, d, d), d ** -0.5)
    inputs['w_o'] = nrm((DEPTH, d, d), d ** -0.5)
    inputs['g_mem_post'] = gain()
    inputs['g_ff2_pre'] = gain()
    inputs['ff2_w_gate'] = nrm((DEPTH, d, D_FF), d ** -0.5)
    inputs['ff2_w_up'] = nrm((DEPTH, d, D_FF), d ** -0.5)
    inputs['ff2_w_down'] = nrm((DEPTH, D_FF, d), D_FF ** -0.5)
    inputs['g_ff2_post'] = gain()
    return inputs


def reference(x_prompt, x_sample, mem_prompt, state_lru_conv, state_lru_h, state_conf_conv,
              cache_mem_k, cache_mem_v, g_ff1_pre, ff1_w_gate, ff1_w_up, ff1_w_down, g_ff1_post,
              g_mix_pre, w_in, lru_conv_w, lru_conv_b, lru_w_a, lru_b_a, lru_w_x, lru_b_x, lru_lambda,
              conf_conv_w, conf_conv_b, conf_ln_g, conf_ln_b, w_out, g_mix_post,
              g_mem_pre, g_mem_kv, w_mem_k, w_mem_v, w_q, w_o, g_mem_post,
              g_ff2_pre, ff2_w_gate, ff2_w_up, ff2_w_down, g_ff2_post):
    xp, xs = x_prompt, x_sample
    bp = xp.shape[0]
    lc_p, lh_p, cc_p, mk_p_all, mv_p_all = [], [], [], [], []
    lc_s, lh_s, cc_s = [], [], []
    for l in range(DEPTH):
        p = dict(
            g_ff1_pre=g_ff1_pre[l], ff1_w_gate=ff1_w_gate[l], ff1_w_up=ff1_w_up[l],
            ff1_w_down=ff1_w_down[l], g_ff1_post=g_ff1_post[l],
            g_mix_pre=g_mix_pre[l], w_in=w_in[l], lru_conv_w=lru_conv_w[l], lru_conv_b=lru_conv_b[l],
            lru_w_a=lru_w_a[l], lru_b_a=lru_b_a[l], lru_w_x=lru_w_x[l], lru_b_x=lru_b_x[l],
            lru_lambda=lru_lambda[l], conf_conv_w=conf_conv_w[l], conf_conv_b=conf_conv_b[l],
            conf_ln_g=conf_ln_g[l], conf_ln_b=conf_ln_b[l], w_out=w_out[l], g_mix_post=g_mix_post[l],
            g_mem_pre=g_mem_pre[l], w_q=w_q[l], w_o=w_o[l], g_mem_post=g_mem_post[l],
            g_ff2_pre=g_ff2_pre[l], ff2_w_gate=ff2_w_gate[l], ff2_w_up=ff2_w_up[l],
            ff2_w_down=ff2_w_down[l], g_ff2_post=g_ff2_post[l])
        mk_p, mv_p = memory_kv(mem_prompt, g_mem_kv[l], w_mem_k[l], w_mem_v[l])
        zero_c4 = jnp.zeros((bp, LRU_CONV_W - 1, D_LRU), xp.dtype)
        zero_h = jnp.zeros((bp, D_LRU), xp.dtype)
        zero_c31 = jnp.zeros((bp, CONF_CONV_W - 1, D_CONV), xp.dtype)
        xp, c4p, hp, c31p = decoder_layer(xp, zero_c4, zero_h, zero_c31, mk_p, mv_p, p)
        xs, c4s, hs, c31s = decoder_layer(xs, state_lru_conv[l], state_lru_h[l], state_conf_conv[l],
                                          cache_mem_k[l], cache_mem_v[l], p)
        lc_p.append(c4p); lh_p.append(hp); cc_p.append(c31p)
        mk_p_all.append(mk_p); mv_p_all.append(mv_p)
        lc_s.append(c4s); lh_s.append(hs); cc_s.append(c31s)
    return (xp, xs,
            jnp.stack(lc_p), jnp.stack(lh_p), jnp.stack(cc_p), jnp.stack(mk_p_all), jnp.stack(mv_p_all),
            jnp.stack(lc_s), jnp.stack(lh_s), jnp.stack(cc_s))
```

```python
from contextlib import ExitStack
import numpy as np
import concourse.bass as bass
import concourse.mybir as mybir
from concourse.bass_utils import run_bass_kernel_spmd

F32 = mybir.dt.float32
BF16 = mybir.dt.bfloat16
AF = mybir.ActivationFunctionType
ALU = mybir.AluOpType

NCORES = 8
D = 1024
DFF = 2816
NFC = 22
SEQ = 2048
NS = 128
NREQ = 16
NTOK = SEQ + NS
NMEM = 256
EPS = 1e-6

VEC_NAMES = ["g_ff1_pre", "g_ff1_post", "g_mix_pre", "g_mix_post", "g_mem_pre", "g_mem_kv", "g_mem_post",
             "g_ff2_pre", "g_ff2_post", "lru_conv_b", "lru_b_a", "lru_b_x", "lru_lambda", "conf_conv_b",
             "conf_ln_g", "conf_ln_b"]
VI = {n: i for i, n in enumerate(VEC_NAMES)}
V_W4 = len(VEC_NAMES)
V_W31 = V_W4 + 4
NV = V_W31 + 31


class Sched:
    ENGS = ("pe", "act", "dve", "pool", "sp")

    def __init__(self, nc, stack):
        self.nc = nc
        self.stack = stack
        self.prog = {e: [] for e in self.ENGS}
        self.esem = {e: stack.enter_context(nc.semaphore("es_" + e)) for e in self.ENGS}
        self.ecount = {e: 0 for e in self.ENGS}
        self.waited = {e: {} for e in self.ENGS}
        self.last_write = {}
        self.readers = {}
        self.dma_sems = {}
        self.ps_unread = set()

    def _deps(self, reads, writes):
        deps = []
        for r in reads:
            t = self.last_write.get(r)
            if t is not None:
                deps.append(t)
        for w in writes:
            t = self.last_write.get(w)
            if t is not None:
                deps.append(t)
            deps.extend(self.readers.get(w, {}).values())
        return deps

    def _emit_waits(self, E, deps):
        best = {}
        for (sem, val) in deps:
            k = id(sem)
            if k not in best or best[k][1] < val:
                best[k] = (sem, val)
        for k, (sem, val) in best.items():
            if self.waited[E].get(k, 0) >= val:
                continue
            self.waited[E][k] = val
            self.prog[E].append(("wait", sem, val))

    def _record(self, tok, reads, writes):
        for r in reads:
            d = self.readers.setdefault(r, {})
            k = id(tok[0])
            if k not in d or d[k][1] < tok[1]:
                d[k] = tok
        for w in writes:
            self.last_write[w] = tok
            self.readers[w] = {}

    def op(self, E, fn, reads=(), writes=()):
        psr = [r for r in reads if isinstance(r, tuple) and r[0] == "ps"]
        if E == "pe":
            self.ps_unread.update(w[1] for w in writes if isinstance(w, tuple) and w[0] == "ps")
        else:
            self.ps_unread.difference_update(r[1] for r in psr)
        if psr:
            reads = [r for r in reads if not (isinstance(r, tuple) and r[0] == "ps")]
            writes = list(writes) + psr
        deps = self._deps(reads, writes)
        if E == "pe":
            own = id(self.esem["pe"])
            deps = [d for d in deps if id(d[0]) != own]
        self._emit_waits(E, deps)
        self.ecount[E] += 1
        tok = (self.esem[E], self.ecount[E])
        self.prog[E].append(("op", fn, tok))
        self._record(tok, reads, writes)
        return tok

    def dma(self, Q, out, in_, key, reads=(), writes=()):
        self._emit_waits(Q, self._deps(reads, writes))
        if key not in self.dma_sems:
            self.dma_sems[key] = [self.stack.enter_context(self.nc.semaphore("ds%d" % len(self.dma_sems))), 0]
        ent = self.dma_sems[key]
        ent[1] += 16
        tok = (ent[0], ent[1])

        def fn(eng, out=out, in_=in_):
            return eng.dma_start(out=out, in_=in_)
        self.prog[Q].append(("dma", fn, tok))
        self._record(tok, reads, writes)
        return tok

    def final_wait_all(self, E="sp"):
        toks = list(self.last_write.values())
        for d in self.readers.values():
            toks.extend(d.values())
        self._emit_waits(E, toks)

    def emit(self):
        nc = self.nc
        with nc.Block() as block:
            deco = {"pe": block.tensor, "act": block.scalar, "dve": block.vector, "pool": block.gpsimd, "sp": block.sync}
            for E in self.ENGS:
                items = self.prog[E]

                def body(eng, items=items):
                    for it in items:
                        if it[0] == "wait":
                            eng.wait_ge(it[1], it[2])
                        elif it[0] == "op":
                            it[1](eng).then_inc(it[2][0], 1)
                        else:
                            it[1](eng).then_inc(it[2][0], 16)
                deco[E](body)


def build_program():
    nc = bass.Bass("TRN2", target_bir_lowering=False)

    def din(name, shape, dt=F32):
        return nc.dram_tensor(name, list(shape), dt, kind="ExternalInput").ap()

    def dout(name, shape):
        return nc.dram_tensor(name, list(shape), F32, kind="ExternalOutput").ap()

    x_in = din("x_in", [128, 8, NTOK])
    vecs_d = din("vecs", [128, NV * 8])
    ident_d = din("ident", [128, 128])
    wgu_d = [din("wgu1", [NFC, 128, 2048]), din("wgu2", [NFC, 128, 2048])]
    wd_d = [din("wd1", [8, 128, NFC * 128]), din("wd2", [8, 128, NFC * 128])]
    win_d = din("win", [8, 128, 4096])
    wlru_d = din("wlru", [128, 2048])
    wout_d = din("wout", [8, 128, 2048])
    wq_d = din("wq", [8, 128, 1024])
    wo_d = din("wo", [8, 128, 1024])
    wk_d = din("wk", [8, 128, 1024])
    wv_d = din("wv", [2, 128, 4096])
    memT_d = din("memT", [128, 8 * NMEM])
    zs_d = din("zs_state", [128, 8 * NREQ * 3])
    h0_d = din("h0_state", [128, 8 * NREQ])
    gs_d = din("gs_state", [128, 8 * NREQ * 30])
    gst_d = din("gs_tail", [128, 8 * NREQ * 22])
    kTs_d = din("kTs", [NREQ, 128, 2048])
    vs_d = din("vs", [NREQ, 128, 2048])

    y_out = dout("y_out", [128, 8, NTOK])
    o_lc_p = dout("o_lc_p", [128, 8 * 3])
    o_lh_p = dout("o_lh_p", [128, 8])
    o_cc_p = dout("o_cc_p", [128, 8 * 30])
    o_k_p = dout("o_k_p", [128, 8 * NMEM])
    o_v_p = dout("o_v_p", [128, 2 * D])
    o_lc_s = dout("o_lc_s", [128, 8 * NREQ * 3])
    o_lh_s = dout("o_lh_s", [128, 8 * NREQ])
    o_cc_s_old = dout("o_cc_s_old", [128, 8 * NREQ * 22])
    o_cc_s_new = dout("o_cc_s_new", [128, 8 * NREQ * 8])

    def dscr(name, shape):
        return nc.dram_tensor(name, list(shape), BF16, kind="Internal").ap()

    kTs_s = dscr("kTs_s", [NREQ, 128, 2048])
    vs_s = dscr("vs_s", [NREQ, 128, 2048])

    with ExitStack() as st:
        S = Sched(nc, st)

        def sb(name, shape, dt):
            return nc.alloc_sbuf_tensor(name, list(shape), dt)

        class Seg:
            pass

        def mkseg(sid, N, sample):
            g = Seg()
            g.sid, g.N, g.sample = sid, N, sample
            g.x = sb("x_%s" % sid, [128, 8, N], F32)
            g.hT = sb("hT_%s" % sid, [128, 8, N], BF16)
            g.big = sb("big_%s" % sid, [128, NFC, N], BF16)
            g.yg = sb("yg_%s" % sid, [128, 8, N], F32)
            g.sq = sb("sq_%s" % sid, [128, 8, N], BF16)
            g.cb = sb("cb_%s" % sid, [128, 8, N], BF16)
            g.sd = sb("sd_%s" % sid, [128, N], F32)
            g.rstd = sb("rstd_%s" % sid, [128, N], F32)
            g.TX = sb("TX_%s" % sid, [128, 8, N], F32)
            g.T = [g.TX[:, i, :] for i in range(6)]
            g.xc = [g.TX[:, 6 + i, :] for i in range(2)]
            g.gl = [sb("gl%d_%s" % (i, sid), [128, N], F32) for i in range(2)]
            g.tg = [sb("tg%d_%s" % (i, sid), [128, N], F32) for i in range(2)]
            g.xcb = sb("xcb_%s" % sid, [128, N], BF16)
            return g

        P = mkseg("p", 512, False)
        Sg = mkseg("s", NS, True)

        def R(g, name, *idx):
            return (name, g.sid) + tuple(idx)

        def TXR(g, d):
            return R(g, "T", d) if d < 6 else R(g, "xc", d - 6)

        NRING = 5
        ring = [sb("ring%d" % i, [128, 2048], BF16) for i in range(NRING)]
        vecs = sb("vecs_sb", [128, NV, 8], F32)
        vh = sb("vh", [128, 2, 8], F32)
        cA = sb("cA", [128, 8], F32)
        cAh = sb("cAh", [128, 8], F32)
        tmp8 = sb("tmp8", [128, 8], F32)
        w31h = sb("w31h", [128, 31, 8], F32)
        hb = sb("hb", [128, 2, 8], F32)
        ident32 = sb("ident32", [128, 128], F32)
        identb = sb("identb", [128, 128], BF16)
        ones = sb("ones", [128, 128], BF16)
        epsb = sb("epsb", [128, 1], F32)
        oneb = sb("oneb", [128, 1], F32)
        wlru = sb("wlru_sb", [128, 8, 2, 128], BF16)
        d4 = sb("d4", [128, 4, 128], BF16)
        d31 = sb("d31", [128, 31, 128], BF16)
        kT_p = sb("kT_p", [128, 8, NMEM], BF16)
        v_p = sb("v_p", [128, 2, D], BF16)
        ygf = P.yg[:].rearrange("p c n -> p (c n)")
        mem32 = ygf[:, 0:2048].rearrange("p (c m) -> p c m", c=8)
        stage32 = ygf[:, 2048:4096]
        YGALL = [R(P, "yg", d) for d in range(8)]
        Zx = [sb("Zx%d" % i, [128, 3 + 512], BF16) for i in range(2)]
        Gb = [sb("Gb%d" % i, [128, 30 + 512], BF16) for i in range(2)]
        zx_carry = sb("zx_carry", [128, 8, 3], BF16)
        g_carry = sb("g_carry", [128, 8, 30], BF16)
        h_carry = sb("h_carry", [128, 8], F32)
        lc32 = sb("lc32", [128, 8, 3], F32)
        cc32 = sb("cc32", [128, 8, 30], F32)
        PT = [sb("PT%d" % i, [128, 2, 512], BF16) for i in range(2)]
        Zs = sb("Zs", [128, NREQ, 11], BF16)
        Gs = sb("Gs", [128, NREQ, 38], BF16)
        zs32 = sb("zs32", [128, 8, NREQ, 3], F32)
        h0 = sb("h0", [128, 8, NREQ], F32)
        glus32 = sb("glus32", [128, 8, NREQ, 8], F32)
        gsb = sb("gsb", [128, 8, NREQ, 30], BF16)
        lcs32 = sb("lcs32", [128, 8, NREQ, 3], F32)
        lhs32 = sb("lhs32", [128, 8, NREQ], F32)
        tmp16 = sb("tmp16", [128, NREQ], F32)
        PTs = [sb("PTs%d" % i, [128, 64], BF16) for i in range(2)]

        ps = [nc.alloc_psum_tensor("ps%d" % i, [128, 512], F32) for i in range(8)]
        state = {"bank": 0, "pool": list(range(8)), "ring": 0, "held": set(), "si": 0}

        def newbank():
            while True:
                b = state["pool"][state["bank"] % len(state["pool"])]
                state["bank"] += 1
                if b not in state["held"]:
                    assert b not in S.ps_unread, "PSUM bank %d re-allocated before its consumer was emitted" % b
                    return b

        def load_slab(src2d, width, scr=None, skey=None):
            assert width <= 2048
            i = state["ring"] % NRING
            state["ring"] += 1
            if scr is None:
                S.dma("pool", ring[i][:, :width], src2d, key=("ring", i), reads=state.pop("first_dep", []), writes=[("ring", i)])
            elif state["si"] == 0:
                S.dma("pool", ring[i][:, :width], src2d, key=("ring", i), writes=[("ring", i)])
                S.dma("sp", scr, ring[i][:, :width], key=("scrst", i), reads=[("ring", i)], writes=[("scr", skey)])
            else:
                extra = ["kvcast_all"] if skey[0] in ("k", "v") else []
                S.dma("pool", ring[i][:, :width], scr, key=("ring", i), reads=[("scr", skey)] + extra, writes=[("ring", i)])
            return ring[i], ("ring", i)

        def V(name, c):
            return vecs[:, VI[name], c:c + 1]

        warm_t = sb("warm_t", [128, 2], F32)
        S.op("dve", lambda e: e.memset(warm_t[:], 1.0), writes=["warm"])

        def act_warm(func):
            S.op("act", lambda e: e.activation(out=warm_t[:, 1:2], in_=warm_t[:, 0:1], func=func), reads=["warm"], writes=["warm_o"])

        def r8(ap):
            return ap.rearrange("p (r t) -> p r t", t=8)

        def mm_group(bank, N, pairs, reads, out=None, fine=None, first=True, last=True):
            o = ps[bank][:, :N] if out is None else out
            if fine is not None:
                n = len(pairs)
                tok = None
                for k, (l, r) in enumerate(pairs):
                    tok = S.op("pe", lambda pe, l=l, r=r, k=k: pe.matmul(o, l, r, start=(first and k == 0), stop=(last and k == n - 1)),
                               reads=list(reads) + list(fine[k]), writes=[("ps", bank)])
                return tok

            def fn(pe, pairs=pairs, o=o):
                n = len(pairs)
                ins = None
                for k, (l, r) in enumerate(pairs):
                    ins = pe.matmul(o, l, r, start=(first and k == 0), stop=(last and k == n - 1))
                return ins
            return S.op("pe", fn, reads=reads, writes=[("ps", bank)])

        def stats_to_rstd(g, scale):
            N = g.N
            b = newbank()
            act_warm(AF.Ln)
            for d in range(8):
                S.op("pe", lambda pe, d=d, b=b: pe.matmul(ps[b][:, :N], ones[:, :], g.sq[:, d, :N], start=(d == 0), stop=(d == 7)),
                     reads=[R(g, "sq", d), "ones"], writes=[("ps", b)])
            S.op("act", lambda e: e.activation(out=g.sd[:, :N], in_=ps[b][:, :N], func=AF.Ln, bias=epsb[:, :], scale=scale),
                 reads=[("ps", b), "epsb"], writes=[R(g, "sd")])
            S.op("act", lambda e: e.activation(out=g.rstd[:, :N], in_=g.sd[:, :N], func=AF.Exp, scale=-0.5), reads=[R(g, "sd")], writes=[R(g, "rstd")])

        def prenorm(g, gname, xb=None, N=None, xregs=None, xr=None):
            xb = g.x if xb is None else xb
            N = g.N if N is None else N
            if xr is None:
                xr = (lambda d: xregs) if xregs is not None else (lambda d: [R(g, "x", d)])
            for d in range(8):
                S.op("act", lambda e, d=d: e.activation(out=g.sq[:, d, :N], in_=xb[:, d, :N], func=AF.Square), reads=xr(d), writes=[R(g, "sq", d)])
            b = newbank()
            for d in range(8):
                S.op("pe", lambda pe, d=d, b=b: pe.matmul(ps[b][:, :N], ones[:, :], g.sq[:, d, :N], start=(d == 0), stop=(d == 7)),
                     reads=[R(g, "sq", d), "ones"], writes=[("ps", b)])
            S.op("act", lambda e: e.activation(out=g.sd[:, :N], in_=ps[b][:, :N], func=AF.Ln, bias=epsb[:, :], scale=1.0 / D),
                 reads=[("ps", b), "epsb"], writes=[R(g, "sd")])
            S.op("act", lambda e: e.activation(out=g.rstd[:, :N], in_=g.sd[:, :N], func=AF.Exp, scale=-0.5), reads=[R(g, "sd")], writes=[R(g, "rstd")])
            for d in range(8):
                S.op("dve", lambda e, d=d: e.scalar_tensor_tensor(out=g.hT[:, d, :N], in0=xb[:, d, :N], scalar=V(gname, d),
                                                                   in1=g.rstd[:, :N], op0=ALU.mult, op1=ALU.mult),
                     reads=xr(d) + [R(g, "rstd"), "vecs"], writes=[R(g, "h", d)])

        def postnorm(segs, gap, producer, bank_of=None, xsrc=None, on_x_done=None, after_d=None):
            for d in range(8):
                prod = producer(d)
                for g in segs:
                    N = g.N
                    b = newbank() if bank_of is None else bank_of(g, d)
                    prod(g, b)
                    S.op("act", lambda e, d=d, b=b, g=g, N=N: e.activation(out=g.sq[:, d, :N], in_=ps[b][:, :N], func=AF.Square),
                         reads=[("ps", b)], writes=[R(g, "sq", d)])
                    S.op("dve", lambda e, d=d, b=b, g=g, N=N: e.tensor_scalar(out=g.yg[:, d, :N], in0=ps[b][:, :N], scalar1=gap(d), scalar2=None,
                                                                             op0=ALU.mult),
                         reads=[("ps", b), "vecs"], writes=[R(g, "yg", d)])
                if after_d is not None:
                    after_d(d)
            for g in segs:
                stats_to_rstd(g, 1.0 / D)
            for g in segs:
                N = g.N
                for d in range(8):
                    S.op("dve", lambda e, d=d, g=g, N=N: e.tensor_tensor(out=g.yg[:, d, :N], in0=g.yg[:, d, :N], in1=g.rstd[:, :N], op=ALU.mult),
                         reads=[R(g, "yg", d), R(g, "rstd")], writes=[R(g, "yg", d)])
                    if xsrc is not None and xsrc(g, d) is not None:
                        xin, xreg = xsrc(g, d)
                    else:
                        xin, xreg = g.x[:, d, :N], R(g, "x", d)
                    S.op("dve", lambda e, d=d, g=g, N=N, xin=xin: e.tensor_tensor(out=g.x[:, d, :N], in0=xin, in1=g.yg[:, d, :N], op=ALU.add),
                         reads=[R(g, "yg", d), xreg], writes=[R(g, "x", d)])
                    if on_x_done is not None:
                        on_x_done(g, d)

        def ffn(segs, li, on_x_done=None, mid_hook=None, after_d=None):
            pre = "g_ff1_pre" if li == 0 else "g_ff2_pre"
            for g in segs:
                if li == 0 and not g.sample and state.pop("early_pn", False):
                    continue
                if li == 0 and g.sample and state.pop("early_pn_s", False):
                    continue
                if li == 0 and not g.sample:
                    prenorm(g, pre, xb=g.TX, xr=lambda d, g=g: [TXR(g, d)])
                else:
                    prenorm(g, pre)
            xsrc = None
            if li == 0:
                xsrc = lambda g, d: None if g.sample else (g.TX[:, d, :g.N], TXR(g, d))
            for j in range(NFC):
                slab, rk = load_slab(wgu_d[li][j], 2048)
                sv = slab[:, :2048].rearrange("p (s k f) -> p s k f", s=2, k=8)
                for g in segs:
                    N = g.N
                    hreads = [R(g, "h", d) for d in range(8)]
                    bg, bu = newbank(), newbank()
                    if j == 0:
                        mm_group(bg, N, [(sv[:, 0, k, :], g.hT[:, k, :N]) for k in range(8)], reads=[rk], fine=[[R(g, "h", k)] for k in range(8)])
                    else:
                        mm_group(bg, N, [(sv[:, 0, k, :], g.hT[:, k, :N]) for k in range(8)], reads=hreads + [rk])
                    mm_group(bu, N, [(sv[:, 1, k, :], g.hT[:, k, :N]) for k in range(8)], reads=hreads + [rk])
                    sg = g.tg[j % 2]
                    S.op("act", lambda e, bg=bg, sg=sg, N=N: e.activation(out=sg[:, :N], in_=ps[bg][:, :N], func=AF.Silu),
                         reads=[("ps", bg)], writes=[R(g, "tg", j % 2)])
                    S.op("dve", lambda e, bu=bu, sg=sg, j=j, g=g, N=N: e.tensor_tensor(out=g.big[:, j, :N], in0=sg[:, :N], in1=ps[bu][:, :N], op=ALU.mult),
                         reads=[("ps", bu), R(g, "tg", j % 2)], writes=[R(g, "big", j)])

            if mid_hook is not None:
                mid_hook()

            def producer(d):
                slabs = {}
                slab, rk = load_slab(wd_d[li][d][:, 0:2048], 2048)
                sva = slab[:, :2048].rearrange("p (k f) -> p k f", k=16)
                slab2, rk2 = load_slab(wd_d[li][d][:, 2048:2816], 768)
                svb = slab2[:, :768].rearrange("p (k f) -> p k f", k=6)
                wsl = [sva[:, k, :] for k in range(16)] + [svb[:, k, :] for k in range(6)]

                def fn(g, b):
                    N = g.N
                    pairs = [(wsl[k], g.big[:, k, :N]) for k in range(NFC)]
                    if d == 0:
                        mm_group(b, N, pairs, reads=[rk, rk2], fine=[[R(g, "big", k)] for k in range(NFC)])
                    else:
                        mm_group(b, N, pairs, reads=[R(g, "big", k) for k in range(NFC)] + [rk, rk2])
                return fn
            postnorm(segs, lambda d: vh[:, li, d:d + 1], producer, xsrc=xsrc, on_x_done=on_x_done, after_d=after_d)

        def early_prenorm_parts(g, gname):
            N = g.N

            def part_squares():
                for d in range(8):
                    S.op("act", lambda e, d=d: e.activation(out=g.cb[:, d, :N], in_=g.TX[:, d, :N], func=AF.Square),
                         reads=[TXR(g, d)], writes=[R(g, "cb", d)])

            def part_rest():
                b = newbank()
                for d in range(8):
                    S.op("pe", lambda pe, d=d, b=b: pe.matmul(ps[b][:, :N], ones[:, :], g.cb[:, d, :N], start=(d == 0), stop=(d == 7)),
                         reads=[R(g, "cb", d), "ones"], writes=[("ps", b)])
                S.op("act", lambda e: e.activation(out=g.sd[:, :N], in_=ps[b][:, :N], func=AF.Ln, bias=epsb[:, :], scale=1.0 / D),
                     reads=[("ps", b), "epsb"], writes=[R(g, "sd")])
                S.op("act", lambda e: e.activation(out=g.rstd[:, :N], in_=g.sd[:, :N], func=AF.Exp, scale=-0.5), reads=[R(g, "sd")], writes=[R(g, "rstd")])
                for d in range(8):
                    S.op("dve", lambda e, d=d: e.scalar_tensor_tensor(out=g.hT[:, d, :N], in0=g.TX[:, d, :N], scalar=V(gname, d),
                                                                       in1=g.rstd[:, :N], op0=ALU.mult, op1=ALU.mult),
                         reads=[TXR(g, d), R(g, "rstd"), "vecs"], writes=[R(g, "h", d)])
            return part_squares, part_rest

        TX_ALL = [TXR(P, d) for d in range(8)]
        S.dma("sp", vecs[:].rearrange("p v c -> p (v c)"), vecs_d, key="vecs", writes=["vecs"])
        S.dma("sp", P.TX[:, 0:4, :], x_in[:, 0:4, 0:512], key="xin_p", writes=TX_ALL[0:4])
        S.dma("act", P.TX[:, 4:8, :], x_in[:, 4:8, 0:512], key="xin_p2", writes=TX_ALL[4:8])
        state["first_dep"] = list(TX_ALL)
        S.dma("sp", ident32[:], ident_d, key="ident", writes=["ident32"])
        S.op("dve", lambda e: e.memset(epsb[:], EPS), writes=["epsb"])
        S.op("dve", lambda e: e.memset(oneb[:], 1.0), writes=["oneb"])
        S.op("dve", lambda e: e.memset(ones[:], 1.0), writes=["ones"])
        S.op("dve", lambda e: e.tensor_scalar(out=w31h[:], in0=vecs[:, V_W31:V_W31 + 31, :], scalar1=0.5, scalar2=None, op0=ALU.mult), reads=["vecs"], writes=["w31h"])
        S.op("dve", lambda e: e.tensor_scalar(out=hb[:, 0, :], in0=vecs[:, VI["lru_b_a"], :], scalar1=0.5, scalar2=None, op0=ALU.mult), reads=["vecs"], writes=["hb"])
        S.op("dve", lambda e: e.tensor_scalar(out=hb[:, 1, :], in0=vecs[:, VI["lru_b_x"], :], scalar1=0.5, scalar2=None, op0=ALU.mult), reads=["vecs"], writes=["hb"])
        S.op("dve", lambda e: e.tensor_copy(out=identb[:], in_=ident32[:]), reads=["ident32"], writes=["identb"])
        S.op("dve", lambda e: e.memset(zx_carry[:], 0.0), writes=["zx_carry"])
        S.op("dve", lambda e: e.memset(g_carry[:], 0.0), writes=["g_carry"])
        S.op("dve", lambda e: e.memset(h_carry[:], 0.0), writes=["h_carry"])
        for li, nm in enumerate(["g_ff1_post", "g_ff2_post"]):
            S.op("dve", lambda e, li=li, nm=nm: e.tensor_scalar(out=vh[:, li, :], in0=vecs[:, VI[nm], :], scalar1=0.5, scalar2=None, op0=ALU.mult),
                 reads=["vecs"], writes=["vecs"])
        S.op("act", lambda e: e.activation(out=tmp8[:], in_=vecs[:, VI["lru_lambda"], :], func=AF.Exp, scale=-1.0), reads=["vecs"], writes=["tmp8"])
        S.op("act", lambda e: e.activation(out=tmp8[:], in_=tmp8[:], func=AF.Ln, bias=oneb[:, :], scale=1.0), reads=["tmp8", "oneb"], writes=["tmp8"])
        S.op("dve", lambda e: e.tensor_scalar(out=cA[:], in0=tmp8[:], scalar1=-8.0, scalar2=None, op0=ALU.mult), reads=["tmp8"], writes=["cA"])
        S.op("dve", lambda e: e.tensor_scalar(out=cAh[:], in0=tmp8[:], scalar1=-4.0, scalar2=None, op0=ALU.mult), reads=["tmp8"], writes=["cA"])
        S.dma("sp", zs32[:].rearrange("p c r t -> p (c r t)"), zs_d, key="zs32", writes=["zs32"])
        S.dma("sp", h0[:].rearrange("p c r -> p (c r)"), h0_d, key="h0", writes=["h0"])
        xsf = Sg.x[:].rearrange("p c n -> p (c n)")
        XS_ALL = [R(Sg, "x", d) for d in range(8)]
        def gs_tail_bounce():
            for part in range(3):
                w0, w1 = part * 1024, min((part + 1) * 1024, 8 * NREQ * 22)
                S.dma("sp", xsf[:, 0:w1 - w0], gst_d[:, w0:w1], key="gst_in", writes=XS_ALL)
                S.dma("sp", o_cc_s_old[:, w0:w1], xsf[:, 0:w1 - w0], key="gst_out", reads=XS_ALL)

        def mem_kv():
            S.dma("sp", ygf[:, 0:2048], memT_d, key="mem", writes=YGALL)
            prenorm(P, "g_mem_kv", xb=mem32, N=NMEM, xregs=YGALL)
            mreads = [R(P, "h", d) for d in range(8)]
            for o in range(8):
                slab, rk = load_slab(wk_d[o], 1024)
                sv = slab[:, :1024].rearrange("p (k f) -> p k f", k=8)
                b = newbank()
                mm_group(b, NMEM, [(sv[:, k, :], P.hT[:, k, :NMEM]) for k in range(8)], reads=mreads + [rk])
                S.op("act", lambda e, o=o, b=b: e.activation(out=stage32[:, o * NMEM:(o + 1) * NMEM], in_=ps[b][:, :NMEM], func=AF.Copy),
                     reads=[("ps", b)], writes=YGALL)
                S.op("dve", lambda e, o=o, b=b: e.tensor_copy(out=kT_p[:, o, :], in_=ps[b][:, :NMEM]), reads=[("ps", b)], writes=["kT_p"])
            S.dma("sp", o_k_p, stage32[:, :], key="o_k_p", reads=YGALL)
            for half in range(2):
                svs = []
                for q in range(2):
                    slab, rk = load_slab(wv_d[half][:, q * 2048:(q + 1) * 2048], 2048)
                    svs.append((slab[:, :2048].rearrange("p (k f) -> p k f", k=4), rk))
                for mc in range(2):
                    b = newbank()
                    mm_group(b, 512, [(P.hT[:, k, mc * 128:(mc + 1) * 128], svs[k // 4][0][:, k % 4, :]) for k in range(8)],
                             reads=mreads + [svs[0][1], svs[1][1]])
                    S.op("act", lambda e, b=b, mc=mc, half=half: e.activation(out=stage32[:, mc * 1024 + half * 512: mc * 1024 + half * 512 + 512],
                                                                            in_=ps[b][:, :], func=AF.Copy),
                         reads=[("ps", b)], writes=YGALL)
                    S.op("dve", lambda e, b=b, mc=mc, half=half: e.tensor_copy(out=v_p[:, mc, half * 512:(half + 1) * 512], in_=ps[b][:, :]),
                         reads=[("ps", b)], writes=["v_p"])
            S.dma("sp", o_v_p, stage32[:, :], key="o_v_p", reads=YGALL)

        def mix_early(g, c, svA, rkA, svB, rkB, out):
            N, sample = g.N, g.sample
            hreads = [R(g, "h", d) for d in range(8)]
            zx, G = Zx[c % 2], Gb[c % 2]
            tg, gl, xc32 = g.tg[0], g.gl[c % 2], g.xc[c % 2]
            b0, b4, b5, b6 = newbank(), newbank(), newbank(), newbank()
            if c == 0:
                mm_group(b6, N, [(svB[:, 1, k, :], g.hT[:, k, :N]) for k in range(8)], reads=[rkB], fine=[[R(g, "h", k)] for k in range(8)])
            else:
                mm_group(b6, N, [(svB[:, 1, k, :], g.hT[:, k, :N]) for k in range(8)], reads=hreads + [rkB])
            mm_group(b5, N, [(svB[:, 0, k, :], g.hT[:, k, :N]) for k in range(8)], reads=hreads + [rkB])
            mm_group(b0, N, [(svA[:, 0, k, :], g.hT[:, k, :N]) for k in range(8)], reads=hreads + [rkA])
            mm_group(b4, N, [(svA[:, 1, k, :], g.hT[:, k, :N]) for k in range(8)], reads=hreads + [rkA])
            if not sample:
                S.op("dve", lambda e: e.tensor_copy(out=zx[:, 0:3], in_=zx_carry[:, c, :]), reads=["zx_carry"], writes=[("zx", c % 2)])
            else:
                S.op("dve", lambda e: e.tensor_copy(out=Zs[:, :, 0:3], in_=zs32[:, c, :, :]), reads=["zs32"], writes=["Zs"])
            yield
            S.op("act", lambda e: e.activation(out=tg[:, :N], in_=ps[b6][:, :N], func=AF.Tanh, scale=0.5), reads=[("ps", b6)], writes=[R(g, "tg", 0)])
            if not sample:
                S.op("dve", lambda e: e.tensor_copy(out=G[:, 0:30], in_=g_carry[:, c, :]), reads=["g_carry"], writes=[("G", c % 2)])
                S.op("dve", lambda e: e.scalar_tensor_tensor(out=G[:, 30:30 + N], in0=tg[:, :N], scalar=1.0, in1=ps[b5][:, :N], op0=ALU.add, op1=ALU.mult),
                     reads=[R(g, "tg", 0), ("ps", b5)], writes=[("G", c % 2)])
                S.op("dve", lambda e: e.scalar_tensor_tensor(out=cc32[:, c, :], in0=tg[:, N - 30:N], scalar=1.0, in1=ps[b5][:, N - 30:N],
                                                             op0=ALU.add, op1=ALU.mult),
                     reads=[R(g, "tg", 0), ("ps", b5)], writes=["cc32"])
                S.op("dve", lambda e: e.tensor_scalar(out=cc32[:, c, :], in0=cc32[:, c, :], scalar1=0.5, scalar2=None, op0=ALU.mult),
                     reads=["cc32"], writes=["cc32"])
                S.op("dve", lambda e: e.tensor_copy(out=g_carry[:, c, :], in_=G[:, N:N + 30]), reads=[("G", c % 2)], writes=["g_carry"])
                conv_rhs = [G[:, k:k + N] for k in range(31)]
                g_reads = [("G", c % 2)]
            else:
                tgv, p5v = r8(tg[:, :N]), r8(ps[b5][:, :N])
                S.op("dve", lambda e: e.tensor_scalar(out=Gs[:, :, 0:30], in0=gsb[:, c, :, :], scalar1=2.0, scalar2=None, op0=ALU.mult),
                     reads=["gsb"], writes=["Gs"])
                S.op("dve", lambda e: e.scalar_tensor_tensor(out=Gs[:, :, 30:38], in0=tgv, scalar=1.0, in1=p5v, op0=ALU.add, op1=ALU.mult),
                     reads=[R(g, "tg", 0), ("ps", b5)], writes=["Gs"])
                S.op("dve", lambda e: e.scalar_tensor_tensor(out=glus32[:, c, :, :], in0=tgv, scalar=1.0, in1=p5v, op0=ALU.add, op1=ALU.mult),
                     reads=[R(g, "tg", 0), ("ps", b5)], writes=["glus32"])
                S.op("dve", lambda e: e.tensor_scalar(out=glus32[:, c, :, :], in0=glus32[:, c, :, :], scalar1=0.5, scalar2=None, op0=ALU.mult),
                     reads=["glus32"], writes=["glus32"])
                conv_rhs = [Gs[:, :, k:k + 8] for k in range(31)]
                g_reads = ["Gs"]
            if not sample:
                S.op("act", lambda e: e.activation(out=zx[:, 3:3 + N], in_=ps[b0][:, :N], func=AF.Copy), reads=[("ps", b0)], writes=[("zx", c % 2)])
                S.op("dve", lambda e: e.tensor_copy(out=lc32[:, c, :], in_=ps[b0][:, N - 3:N]), reads=[("ps", b0)], writes=["lc32"])
                S.op("dve", lambda e: e.tensor_copy(out=zx_carry[:, c, :], in_=zx[:, N:N + 3]), reads=[("zx", c % 2)], writes=["zx_carry"])
                conv4_rhs = [zx[:, k:k + N] for k in range(4)]
                cv_reads = [("zx", c % 2)]
            else:
                psv = r8(ps[b0][:, :N])
                S.op("act", lambda e: e.activation(out=Zs[:, :, 3:11], in_=psv, func=AF.Copy), reads=[("ps", b0)], writes=["Zs"])
                S.op("dve", lambda e: e.tensor_copy(out=lcs32[:, c, :, :], in_=psv[:, :, 5:8]), reads=[("ps", b0)], writes=["lcs32"])
                conv4_rhs = [Zs[:, :, k:k + 8] for k in range(4)]
                cv_reads = ["Zs"]
            S.op("act", lambda e: e.activation(out=gl[:, :N], in_=ps[b4][:, :N], func=AF.Gelu_apprx_tanh), reads=[("ps", b4)], writes=[R(g, "gl", c % 2)])
            yield
            b1 = newbank()
            o1 = r8(ps[b1][:, :N]) if sample else None
            mm_group(b1, N, [(d4[:, k, :], conv4_rhs[k]) for k in range(4)], reads=cv_reads + ["d4"], out=o1)
            S.op("act", lambda e: e.activation(out=xc32[:, :N], in_=ps[b1][:, :N], func=AF.Identity, bias=V("lru_conv_b", c), scale=1.0),
                 reads=[("ps", b1), "vecs"], writes=[R(g, "xc", c % 2)])
            S.op("dve", lambda e: e.tensor_copy(out=g.xcb[:, :N], in_=xc32[:, :N]), reads=[R(g, "xc", c % 2)], writes=[R(g, "xcb")])
            yield
            b7 = newbank()
            o7 = r8(ps[b7][:, :N]) if sample else None
            mm_group(b7, N, [(d31[:, k, :], conv_rhs[k]) for k in range(31)], reads=g_reads + ["d31"], out=o7)
            S.op("act", lambda e: e.activation(out=g.yg[:, c, :N], in_=ps[b7][:, :N], func=AF.Identity, bias=V("conf_conv_b", c), scale=1.0),
                 reads=[("ps", b7), "vecs"], writes=[R(g, "yg", c)])
            S.op("act", lambda e: e.activation(out=g.sq[:, c, :N], in_=g.yg[:, c, :N], func=AF.Square), reads=[R(g, "yg", c)], writes=[R(g, "sq", c)])
            S.op("act", lambda e: e.activation(out=g.cb[:, c, :N], in_=g.yg[:, c, :N], func=AF.Copy), reads=[R(g, "yg", c)], writes=[R(g, "cb", c)])
            yield
            b2, b3 = newbank(), newbank()
            mm_group(b2, N, [(wlru[:, c, 0, :], g.xcb[:, :N])], reads=[R(g, "xcb"), "wlru"])
            mm_group(b3, N, [(wlru[:, c, 1, :], g.xcb[:, :N])], reads=[R(g, "xcb"), "wlru"])
            state["held"].update((b2, b3))
            out[g.sid] = (b2, b3)

        def mix_tail_act(g, c, banks):
            N = g.N
            b2, b3 = banks
            r32, i32, a32, s32 = g.T[1], g.T[2], g.T[3], g.T[4]
            S.op("act", lambda e: e.activation(out=r32[:, :N], in_=ps[b2][:, :N], func=AF.Tanh, bias=hb[:, 0, c:c + 1], scale=0.5),
                 reads=[("ps", b2), "hb"], writes=[R(g, "T", 1)])
            S.op("act", lambda e: e.activation(out=i32[:, :N], in_=ps[b3][:, :N], func=AF.Tanh, bias=hb[:, 1, c:c + 1], scale=0.5),
                 reads=[("ps", b3), "hb"], writes=[R(g, "T", 2)])
            state["held"].difference_update((b2, b3))
            yield
            S.op("act", lambda e: e.activation(out=a32[:, :N], in_=r32[:, :N], func=AF.Exp, bias=cAh[:, c:c + 1], scale=cAh[:, c:c + 1]),
                 reads=[R(g, "T", 1), "cA"], writes=[R(g, "T", 3)])
            S.op("act", lambda e: e.activation(out=s32[:, :N], in_=r32[:, :N], func=AF.Exp, bias=cA[:, c:c + 1], scale=cA[:, c:c + 1]),
                 reads=[R(g, "T", 1), "cA"], writes=[R(g, "T", 4)])
            yield
            S.op("act", lambda e: e.activation(out=s32[:, :N], in_=s32[:, :N], func=AF.Ln, bias=oneb[:, :], scale=-1.0),
                 reads=[R(g, "T", 4), "oneb"], writes=[R(g, "T", 4)])
            yield
            S.op("act", lambda e: e.activation(out=s32[:, :N], in_=s32[:, :N], func=AF.Exp, scale=0.5), reads=[R(g, "T", 4)], writes=[R(g, "T", 4)])

        def mix_tail_dve(g, c):
            N, sample = g.N, g.sample
            i32, a32, s32, hs32 = g.T[2], g.T[3], g.T[4], g.T[5]
            u32 = i32
            xc32, gl = g.xc[c % 2], g.gl[c % 2]
            S.op("dve", lambda e: e.scalar_tensor_tensor(out=u32[:, :N], in0=i32[:, :N], scalar=1.0, in1=xc32[:, :N], op0=ALU.add, op1=ALU.mult),
                 reads=[R(g, "T", 2), R(g, "xc", c % 2)], writes=[R(g, "T", 2)])
            S.op("dve", lambda e: e.scalar_tensor_tensor(out=u32[:, :N], in0=u32[:, :N], scalar=0.5, in1=s32[:, :N], op0=ALU.mult, op1=ALU.mult),
                 reads=[R(g, "T", 2), R(g, "T", 4)], writes=[R(g, "T", 2)])
            if not sample:
                S.op("dve", lambda e: e.tensor_tensor_scan(out=hs32[:, :N], data0=a32[:, :N], data1=u32[:, :N], initial=h_carry[:, c:c + 1],
                                                          op0=ALU.mult, op1=ALU.add),
                     reads=[R(g, "T", 3), R(g, "T", 2), "h_carry"], writes=[R(g, "T", 5)])
                S.op("dve", lambda e: e.tensor_copy(out=h_carry[:, c:c + 1], in_=hs32[:, N - 1:N]), reads=[R(g, "T", 5)], writes=["h_carry"])
            else:
                av, uv = r8(a32[:, :N]), r8(u32[:, :N])
                S.op("dve", lambda e: e.tensor_tensor(out=tmp16[:, :], in0=av[:, :, 0], in1=h0[:, c, :], op=ALU.mult),
                     reads=[R(g, "T", 3), "h0"], writes=["tmp16"])
                S.op("dve", lambda e: e.tensor_tensor(out=uv[:, :, 0], in0=uv[:, :, 0], in1=tmp16[:, :], op=ALU.add),
                     reads=[R(g, "T", 2), "tmp16"], writes=[R(g, "T", 2)])
                S.op("dve", lambda e: e.memset(av[:, :, 0], 0.0), reads=["tmp16"], writes=[R(g, "T", 3)])
                S.op("dve", lambda e: e.tensor_tensor_scan(out=hs32[:, :N], data0=a32[:, :N], data1=u32[:, :N], initial=0.0,
                                                          op0=ALU.mult, op1=ALU.add),
                     reads=[R(g, "T", 3), R(g, "T", 2)], writes=[R(g, "T", 5)])
                hv = r8(hs32[:, :N])
                S.op("dve", lambda e: e.tensor_copy(out=lhs32[:, c, :], in_=hv[:, :, 7]), reads=[R(g, "T", 5)], writes=["lhs32"])
            S.op("dve", lambda e: e.tensor_tensor(out=g.big[:, c, :N], in0=gl[:, :N], in1=hs32[:, :N], op=ALU.mult),
                 reads=[R(g, "gl", c % 2), R(g, "T", 5)], writes=[R(g, "big", c)])

        def mixer(segs):
            for g in segs:
                prenorm(g, "g_mix_pre")
            pend = None

            def run_staged(gens, nstages):
                for _stage in range(nstages):
                    for gen in gens:
                        next(gen, None)

            for c in range(9):
                newp = {}
                gens = []
                if c < 8:
                    slabB, rkB = load_slab(win_d[c][:, 2048:4096], 2048)
                    slabA, rkA = load_slab(win_d[c][:, 0:2048], 2048)
                    svA = slabA[:, :2048].rearrange("p (s k f) -> p s k f", s=2, k=8)
                    svB = slabB[:, :2048].rearrange("p (s k f) -> p s k f", s=2, k=8)
                    S.op("dve", lambda e, c=c: e.tensor_tensor(out=d4[:, :, :], in0=identb[:].unsqueeze(1).broadcast_to([128, 4, 128]),
                                                              in1=vecs[:, V_W4:V_W4 + 4, c:c + 1].broadcast_to([128, 4, 128]), op=ALU.mult),
                         reads=["identb", "vecs"], writes=["d4"])
                    if state["si"] in (1, 2):
                        for r in ((state["si"] - 1) * 8 + c,):
                            S.dma("pool", kTs_s[r], kTs_d[r], key="kvcast", writes=[("scr", ("k", r))])
                            S.dma("pool", vs_s[r], vs_d[r], key="kvcast", writes=[("scr", ("v", r))] + (["kvcast_all"] if r == NREQ - 1 else []))
                    gens = [mix_early(g, c, svA, rkA, svB, rkB, newp) for g in segs]
                    for gi, gen in enumerate(gens):
                        next(gen, None)
                        if gi == 0:
                            S.op("dve", lambda e, c=c: e.tensor_tensor(out=d31[:, :, :], in0=identb[:].unsqueeze(1).broadcast_to([128, 31, 128]),
                                                                      in1=w31h[:, :, c:c + 1].broadcast_to([128, 31, 128]), op=ALU.mult),
                                 reads=["identb", "w31h"], writes=["d31"])
                        next(gen, None)
                if pend is not None:
                    run_staged([mix_tail_act(g, c - 1, pend[g.sid]) for g in segs], 4)
                if c < 8:
                    run_staged(gens, 3)
                    if pend is not None and c < 7:
                        act_warm(AF.Gelu_apprx_tanh)
                if pend is not None:
                    for g in segs:
                        mix_tail_dve(g, c - 1)
                pend = newp if c < 8 else None
            def ln_chain(g):
                N = g.N
                mean32, m2 = g.T[0], g.T[1]
                t1 = [g.T[2], g.T[3]]
                bm, bq = newbank(), newbank()
                mm_group(bm, N, [(ones[:, :], g.cb[:, d, :N]) for d in range(8)], reads=[R(g, "cb", d) for d in range(8)] + ["ones"])
                mm_group(bq, N, [(ones[:, :], g.sq[:, d, :N]) for d in range(8)], reads=[R(g, "sq", d) for d in range(8)] + ["ones"])
                yield
                S.op("act", lambda e: e.activation(out=mean32[:, :N], in_=ps[bm][:, :N], func=AF.Copy, scale=1.0 / D),
                     reads=[("ps", bm)], writes=[R(g, "T", 0)])
                S.op("dve", lambda e: e.tensor_tensor(out=m2[:, :N], in0=mean32[:, :N], in1=mean32[:, :N], op=ALU.mult),
                     reads=[R(g, "T", 0)], writes=[R(g, "T", 1)])
                S.op("dve", lambda e: e.scalar_tensor_tensor(out=m2[:, :N], in0=ps[bq][:, :N], scalar=1.0 / D, in1=m2[:, :N],
                                                             op0=ALU.mult, op1=ALU.subtract),
                     reads=[("ps", bq), R(g, "T", 1)], writes=[R(g, "T", 1)])
                yield
                S.op("act", lambda e: e.activation(out=g.sd[:, :N], in_=m2[:, :N], func=AF.Ln, bias=epsb[:, :], scale=1.0),
                     reads=[R(g, "T", 1), "epsb"], writes=[R(g, "sd")])
                yield
                S.op("act", lambda e: e.activation(out=g.rstd[:, :N], in_=g.sd[:, :N], func=AF.Exp, scale=-0.5), reads=[R(g, "sd")], writes=[R(g, "rstd")])
                yield
                for d in range(8):
                    tt = t1[d % 2]
                    S.op("dve", lambda e, d=d, tt=tt: e.tensor_tensor(out=tt[:, :N], in0=g.yg[:, d, :N], in1=mean32[:, :N], op=ALU.subtract),
                         reads=[R(g, "yg", d), R(g, "T", 0)], writes=[R(g, "T", 2 + d % 2)])
                    S.op("dve", lambda e, tt=tt, d=d: e.tensor_tensor(out=tt[:, :N], in0=tt[:, :N], in1=g.rstd[:, :N], op=ALU.mult),
                         reads=[R(g, "T", 2 + d % 2), R(g, "rstd")], writes=[R(g, "T", 2 + d % 2)])
                    S.op("act", lambda e, tt=tt, d=d: e.activation(out=g.big[:, 8 + d, :N], in_=tt[:, :N], func=AF.Silu,
                                                                  bias=V("conf_ln_b", d), scale=V("conf_ln_g", d)),
                         reads=[R(g, "T", 2 + d % 2), "vecs"], writes=[R(g, "big", 8 + d)])
                    yield

            act_warm(AF.Ln)
            run_staged([ln_chain(g) for g in segs], 12)
            if len(segs) == 1:
                g = segs[0]
                N = g.N
                obank = {}
                for d in range(8):
                    slab, rk = load_slab(wout_d[d][:, 0:1024], 1024)
                    sv = slab[:, :1024].rearrange("p (k f) -> p k f", k=8)
                    obank[d] = newbank()
                    mm_group(obank[d], N, [(sv[:, k, :], g.big[:, k, :N]) for k in range(8)], reads=[R(g, "big", k) for k in range(8)] + [rk],
                             first=True, last=False)

                def producer(d):
                    slab, rk = load_slab(wout_d[d][:, 1024:2048], 1024)
                    sv = slab[:, :1024].rearrange("p (k f) -> p k f", k=8)

                    def fn(g, b):
                        if d == 0:
                            mm_group(b, g.N, [(sv[:, k, :], g.big[:, 8 + k, :g.N]) for k in range(8)], reads=[rk],
                                     fine=[[R(g, "big", 8 + k)] for k in range(8)], first=False, last=True)
                        else:
                            mm_group(b, g.N, [(sv[:, k, :], g.big[:, 8 + k, :g.N]) for k in range(8)], reads=[R(g, "big", 8 + k) for k in range(8)] + [rk],
                                     first=False, last=True)
                    return fn
                postnorm(segs, lambda d: V("g_mix_post", d), producer, bank_of=lambda g, d: obank[d])
            else:
                def producer(d):
                    slab, rk = load_slab(wout_d[d], 2048)
                    sv = slab[:, :2048].rearrange("p (k f) -> p k f", k=16)

                    def fn(g, b):
                        N = g.N
                        mm_group(b, N, [(sv[:, k, :], g.big[:, k, :N]) for k in range(16)], reads=[R(g, "big", k) for k in range(16)] + [rk])
                    return fn
                postnorm(segs, lambda d: V("g_mix_post", d), producer)

        def attention(segs):
            for g in segs:
                prenorm(g, "g_mem_pre")
            for o in range(8):
                slab, rk = load_slab(wq_d[o], 1024)
                sv = slab[:, :1024].rearrange("p (k f) -> p k f", k=8)
                for g in segs:
                    N = g.N
                    b = newbank()
                    mm_group(b, N, [(sv[:, k, :], g.hT[:, k, :N]) for k in range(8)], reads=[R(g, "h", d) for d in range(8)] + [rk])
                    S.op("act", lambda e, o=o, b=b, g=g, N=N: e.activation(out=g.big[:, o, :N], in_=ps[b][:, :N], func=AF.Copy),
                         reads=[("ps", b)], writes=[R(g, "big", o)])
            for g in segs:
                N = g.N
                rinv = g.T[4]
                if not g.sample:
                    def s_mm(h):
                        bs2 = []
                        for mc in range(2):
                            b = newbank()
                            mm_group(b, N, [(kT_p[:, 2 * h + dc, mc * 128:(mc + 1) * 128], g.big[:, 2 * h + dc, :N]) for dc in range(2)],
                                     reads=[R(g, "big", 2 * h), R(g, "big", 2 * h + 1), "kT_p"])
                            bs2.append(b)
                        return bs2

                    def s_exp(h, banks):
                        pt = PT[h % 2]
                        for mc in range(2):
                            b = banks[mc]
                            S.op("act", lambda e, b=b, pt=pt, mc=mc, N=N: e.activation(out=pt[:, mc, :N], in_=ps[b][:, :N], func=AF.Exp, scale=1.0 / 16.0),
                                 reads=[("ps", b)], writes=[("PT", h % 2)])

                    sbanks = {0: s_mm(0)}
                    s_exp(0, sbanks[0])
                    sbanks[1] = s_mm(1)
                    for h in range(4):
                        pt = PT[h % 2]
                        bs = newbank()
                        mm_group(bs, N, [(ones[:, :], pt[:, mc, :N]) for mc in range(2)], reads=[("PT", h % 2), "ones"])
                        pv = []
                        for dch in range(2):
                            b = newbank()
                            mm_group(b, N, [(v_p[:, mc, h * 256 + dch * 128: h * 256 + dch * 128 + 128], pt[:, mc, :N]) for mc in range(2)],
                                     reads=[("PT", h % 2), "v_p"])
                            pv.append(b)
                        S.op("act", lambda e, bs=bs, N=N, rinv=rinv: e.activation(out=rinv[:, :N], in_=ps[bs][:, :N], func=AF.Ln),
                             reads=[("ps", bs)], writes=[R(g, "T", 4)])
                        S.op("act", lambda e, N=N, rinv=rinv: e.activation(out=rinv[:, :N], in_=rinv[:, :N], func=AF.Exp, scale=-1.0),
                             reads=[R(g, "T", 4)], writes=[R(g, "T", 4)])
                        for dch in range(2):
                            b = pv[dch]
                            ch = 8 + 2 * h + dch
                            S.op("dve", lambda e, b=b, ch=ch, g=g, N=N, rinv=rinv: e.tensor_tensor(out=g.big[:, ch, :N], in0=ps[b][:, :N], in1=rinv[:, :N], op=ALU.mult),
                                 reads=[("ps", b), R(g, "T", 4)], writes=[R(g, "big", ch)])
                        if h + 1 < 4:
                            s_exp(h + 1, sbanks[h + 1])
                        if h + 2 < 4:
                            sbanks[h + 2] = s_mm(h + 2)
                else:
                    state["pool"] = [0, 1, 2]
                    BO0, BO1, BSUM = 5, 6, 7
                    BSCS = [3, 4]
                    def sc_stage(r):
                        kslab, kk = load_slab(kTs_d[r], 2048, scr=kTs_s[r], skey=("k", r))
                        kv = kslab[:, :2048].rearrange("p (c m) -> p c m", c=8)
                        BSC = BSCS[r % 2]

                        def fsc(pe, kv=kv, r=r, BSC=BSC, g=g):
                            ins = None
                            for h in range(4):
                                for mc in range(2):
                                    for dc in range(2):
                                        col = (h * 2 + mc) * 8
                                        ins = pe.matmul(ps[BSC][:, col:col + 8], kv[:, 2 * h + dc, mc * 128:(mc + 1) * 128],
                                                        g.big[:, 2 * h + dc, r * 8:(r + 1) * 8], start=(dc == 0), stop=(dc == 1))
                            return ins
                        S.op("pe", fsc, reads=[R(g, "big", o) for o in range(8)] + [kk], writes=[("ps", BSC)])

                    sc_stage(0)
                    for r in range(NREQ):
                        vslab, vk = load_slab(vs_d[r], 2048, scr=vs_s[r], skey=("v", r))
                        vv = vslab[:, :2048].rearrange("p (c f) -> p c f", c=2)
                        sl = r % 2
                        BSC = BSCS[sl]
                        pts = PTs[sl]
                        S.op("act", lambda e, pts=pts, BSC=BSC: e.activation(out=pts[:, :], in_=ps[BSC][:, 0:64], func=AF.Exp, scale=1.0 / 16.0),
                             reads=[("ps", BSC)], writes=[("PTs", sl)])
                        if r + 1 < NREQ:
                            sc_stage(r + 1)

                        def fpv(pe, vv=vv, r=r, pts=pts):
                            ins = None
                            for h in range(4):
                                for dch in range(2):
                                    ch = 2 * h + dch
                                    bank = BO0 if ch < 4 else BO1
                                    col = (ch % 4) * 128 + r * 8
                                    for mc in range(2):
                                        ins = pe.matmul(ps[bank][:, col:col + 8], vv[:, mc, ch * 128:(ch + 1) * 128],
                                                        pts[:, (h * 2 + mc) * 8:(h * 2 + mc) * 8 + 8], start=(mc == 0), stop=(mc == 1))
                                col = h * 128 + r * 8
                                for mc in range(2):
                                    ins = pe.matmul(ps[BSUM][:, col:col + 8], ones[:, :], pts[:, (h * 2 + mc) * 8:(h * 2 + mc) * 8 + 8],
                                                    start=(mc == 0), stop=(mc == 1))
                            return ins
                        S.op("pe", fpv, reads=[("PTs", sl), vk, "ones"], writes=[("ps", BO0), ("ps", BO1), ("ps", BSUM)])
                    rv = P.sd
                    S.op("act", lambda e, rv=rv: e.activation(out=rv[:, :], in_=ps[BSUM][:, :], func=AF.Ln), reads=[("ps", BSUM)], writes=[R(P, "sd")])
                    S.op("act", lambda e, rv=rv: e.activation(out=rv[:, :], in_=rv[:, :], func=AF.Exp, scale=-1.0), reads=[R(P, "sd")], writes=[R(P, "sd")])
                    for ch in range(8):
                        bank = BO0 if ch < 4 else BO1
                        h = ch // 2
                        S.op("dve", lambda e, ch=ch, bank=bank, h=h, g=g, N=N, rv=rv: e.tensor_tensor(
                            out=g.big[:, 8 + ch, :N], in0=ps[bank][:, (ch % 4) * 128:(ch % 4) * 128 + 128], in1=rv[:, h * 128:(h + 1) * 128], op=ALU.mult),
                            reads=[("ps", bank), R(P, "sd")], writes=[R(g, "big", 8 + ch)])

            def producer(d):
                slab, rk = load_slab(wo_d[d], 1024)
                sv = slab[:, :1024].rearrange("p (k f) -> p k f", k=8)

                def fn(g, b):
                    N = g.N
                    if d == 0:
                        mm_group(b, N, [(sv[:, k, :], g.big[:, 8 + k, :N]) for k in range(8)], reads=[rk], fine=[[R(g, "big", 8 + k)] for k in range(8)])
                    else:
                        mm_group(b, N, [(sv[:, k, :], g.big[:, 8 + k, :N]) for k in range(8)], reads=[R(g, "big", 8 + k) for k in range(8)] + [rk])
                return fn
            postnorm(segs, lambda d: V("g_mem_post", d), producer)
            state["pool"] = list(range(8))

        for si in range(4):
            state["si"] = si
            T0 = si * 512
            segs = [P] + ([Sg] if si == 3 else [])
            ffn(segs, 0)
            if si == 0:
                S.dma("pool", wlru[:].rearrange("p h s j -> p (h s j)"), wlru_d, key="wlru", writes=["wlru"])
                S.dma("pool", gsb[:].rearrange("p c r t -> p (c r t)"), gs_d, key="gsb", writes=["gsb"])
                mem_kv()
                gs_tail_bounce()
                S.dma("sp", Sg.x[:, :, :], x_in[:, :, SEQ:SEQ + NS], key="xin_s", writes=XS_ALL)
                prenorm(Sg, "g_ff1_pre")
                state["early_pn_s"] = True
            mixer(segs)
            attention(segs)
            if si + 1 < 4:
                S.dma("sp", P.TX[:, :, :], x_in[:, :, T0 + 512:T0 + 1024], key="xin_p", writes=TX_ALL)
            def store_chunk(g, d, T0=T0):
                if g.sample:
                    S.dma("sp", y_out[:, d, SEQ:SEQ + NS], g.x[:, d, :], key=("yout_s", d), reads=[R(g, "x", d)])
                else:
                    S.dma("sp", y_out[:, d, T0:T0 + 512], g.x[:, d, :], key=("yout_p", d), reads=[R(g, "x", d)])
            if si + 1 < 4:
                p_sq, p_rest = early_prenorm_parts(P, "g_ff1_pre")
                ffn(segs, 1, on_x_done=store_chunk, mid_hook=p_sq, after_d=lambda d, p_rest=p_rest: (p_rest() if d == 1 else None))
                state["early_pn"] = True
            else:
                ffn(segs, 1, on_x_done=store_chunk)
        S.dma("sp", o_lc_p, lc32[:].rearrange("p c t -> p (c t)"), key="o_lc_p", reads=["lc32"])
        S.dma("sp", o_lh_p, h_carry[:, :], key="o_lh_p", reads=["h_carry"])
        S.dma("sp", o_cc_p, cc32[:].rearrange("p c t -> p (c t)"), key="o_cc_p", reads=["cc32"])
        S.dma("sp", o_lc_s, lcs32[:].rearrange("p c r t -> p (c r t)"), key="o_lc_s", reads=["lcs32"])
        S.dma("sp", o_lh_s, lhs32[:].rearrange("p c r -> p (c r)"), key="o_lh_s", reads=["lhs32"])
        S.dma("sp", o_cc_s_new, glus32[:].rearrange("p c r t -> p (c r t)"), key="o_cc_s", reads=["glus32"])
        S.final_wait_all("sp")
        S.emit()
    return nc


def _fm(v):
    return np.ascontiguousarray(v.reshape(8, 128).T)


def _prep_shared(inp):
    f = np.float32
    sh = {}
    vec_list = [inp[n][0] for n in VEC_NAMES] + [inp["lru_conv_w"][0][k] for k in range(4)] + [inp["conf_conv_w"][0][k] for k in range(31)]
    vecs = np.stack([_fm(np.asarray(v, f)) for v in vec_list], axis=1)
    sh["vecs"] = np.ascontiguousarray(vecs.reshape(128, NV * 8))
    sh["ident"] = np.eye(128, dtype=f)

    def gu(wg, wu):
        w = np.stack([wg, wu], 0).reshape(2, 8, 128, NFC, 128)
        return np.ascontiguousarray(w.transpose(3, 2, 0, 1, 4).reshape(NFC, 128, 2048))
    sh["wgu1"] = gu(inp["ff1_w_gate"][0], inp["ff1_w_up"][0])
    sh["wgu2"] = gu(inp["ff2_w_gate"][0], inp["ff2_w_up"][0])

    def dn(wd):
        w = wd.reshape(NFC, 128, 8, 128)
        return np.ascontiguousarray(w.transpose(2, 1, 0, 3).reshape(8, 128, NFC * 128))
    sh["wd1"] = dn(inp["ff1_w_down"][0])
    sh["wd2"] = dn(inp["ff2_w_down"][0])
    w = inp["w_in"][0].reshape(8, 128, 4, 8, 128)
    sh["win"] = np.ascontiguousarray(w.transpose(3, 1, 2, 0, 4).reshape(8, 128, 4096))
    wl = np.stack([inp["lru_w_a"][0], inp["lru_w_x"][0]], 0)
    sh["wlru"] = np.ascontiguousarray(wl.transpose(2, 1, 0, 3).reshape(128, 2048))
    w = inp["w_out"][0].reshape(16, 128, 8, 128)
    sh["wout"] = np.ascontiguousarray(w.transpose(2, 1, 0, 3).reshape(8, 128, 2048))
    for nm, key in (("wq", "w_q"), ("wo", "w_o"), ("wk", "w_mem_k")):
        w = inp[key][0].reshape(8, 128, 8, 128)
        sh[nm] = np.ascontiguousarray(w.transpose(2, 1, 0, 3).reshape(8, 128, 1024))
    w = inp["w_mem_v"][0].reshape(8, 128, 2, 512)
    sh["wv"] = np.ascontiguousarray(w.transpose(2, 1, 0, 3).reshape(2, 128, 4096))
    return sh


def _prep_core(inp, b):
    f = np.float32
    m = {}
    xp = inp["x_prompt"][b]
    xs = inp["x_sample"][16 * b:16 * b + 16].reshape(NS, D)
    xa = np.concatenate([xp, xs], 0)
    m["x_in"] = np.ascontiguousarray(xa.T.reshape(8, 128, NTOK).transpose(1, 0, 2))
    m["memT"] = np.ascontiguousarray(inp["mem_prompt"][b].T.reshape(8, 128, NMEM).transpose(1, 0, 2).reshape(128, 8 * NMEM))
    zs = inp["state_lru_conv"][0, 16 * b:16 * b + 16]
    m["zs_state"] = np.ascontiguousarray(zs.reshape(16, 3, 8, 128).transpose(3, 2, 0, 1).reshape(128, 8 * 16 * 3))
    h0 = inp["state_lru_h"][0, 16 * b:16 * b + 16]
    m["h0_state"] = np.ascontiguousarray(h0.reshape(16, 8, 128).transpose(2, 1, 0).reshape(128, 8 * 16))
    gs = inp["state_conf_conv"][0, 16 * b:16 * b + 16]
    gsl = gs.reshape(16, 30, 8, 128).transpose(3, 2, 0, 1)
    m["gs_state"] = np.ascontiguousarray(gsl.reshape(128, 8 * 16 * 30))
    m["gs_tail"] = np.ascontiguousarray(gsl[:, :, :, 8:30].reshape(128, 8 * 16 * 22))
    k = inp["cache_mem_k"][0, 16 * b:16 * b + 16]
    k = k.reshape(16, NMEM, 4, 2, 128)
    m["kTs"] = np.ascontiguousarray(k.transpose(0, 4, 2, 3, 1).reshape(16, 128, 2048))
    v = inp["cache_mem_v"][0, 16 * b:16 * b + 16].reshape(16, 2, 128, D)
    m["vs"] = np.ascontiguousarray(v.transpose(0, 2, 1, 3).reshape(16, 128, 2048))
    return m


_NC_CACHE = {}


def kernel(**inputs):
    inp = {k: np.asarray(v) for k, v in inputs.items()}
    if "nc" not in _NC_CACHE:
        _NC_CACHE["nc"] = build_program()
    nc = _NC_CACHE["nc"]
    shared = _prep_shared(inp)
    in_maps = []
    for b in range(NCORES):
        m = dict(shared)
        m.update(_prep_core(inp, b))
        in_maps.append(m)
    res = run_bass_kernel_spmd(nc, in_maps, core_ids=list(range(NCORES)))
    R = res.results
    f = np.float32

    def unfm(a):
        return a.transpose(2, 1, 0).reshape(a.shape[2], D)

    y_p = np.stack([unfm(R[b]["y_out"][:, :, :SEQ]) for b in range(NCORES)], 0)
    y_s = np.concatenate([unfm(R[b]["y_out"][:, :, SEQ:]).reshape(16, 8, D) for b in range(NCORES)], 0)
    lc_p = np.stack([unfm(R[b]["o_lc_p"].reshape(128, 8, 3)) for b in range(NCORES)], 0)[None]
    lh_p = np.stack([R[b]["o_lh_p"].T.reshape(D) for b in range(NCORES)], 0)[None]
    cc_p = np.stack([unfm(R[b]["o_cc_p"].reshape(128, 8, 30)) for b in range(NCORES)], 0)[None]
    k_p = np.stack([unfm(R[b]["o_k_p"].reshape(128, 8, NMEM)).reshape(NMEM, 4, 256) for b in range(NCORES)], 0)[None]
    v_p = np.stack([R[b]["o_v_p"].reshape(128, 2, D).transpose(1, 0, 2).reshape(NMEM, 4, 256) for b in range(NCORES)], 0)[None]

    def un_s(a, T):
        return a.transpose(2, 3, 1, 0).reshape(16, T, D)
    lc_s = np.concatenate([un_s(R[b]["o_lc_s"].reshape(128, 8, 16, 3), 3) for b in range(NCORES)], 0)[None]
    lh_s = np.concatenate([R[b]["o_lh_s"].reshape(128, 8, 16).transpose(2, 1, 0).reshape(16, D) for b in range(NCORES)], 0)[None]
    cc_s = np.concatenate([un_s(np.concatenate([R[b]["o_cc_s_old"].reshape(128, 8, 16, 22), R[b]["o_cc_s_new"].reshape(128, 8, 16, 8)], axis=3), 30) for b in range(NCORES)], 0)[None]
    outs = (y_p, y_s, lc_p, lh_p, cc_p, k_p, v_p, lc_s, lh_s, cc_s)
    return tuple(np.ascontiguousarray(o, dtype=f) for o in outs)
```

```python
from contextlib import ExitStack
import numpy as np
import concourse.bass as bass
import concourse.mybir as mybir
from concourse.bass_utils import run_bass_kernel_spmd

F32 = mybir.dt.float32
BF16 = mybir.dt.bfloat16
AF = mybir.ActivationFunctionType
ALU = mybir.AluOpType

NCORES = 8
D = 1024
DFF = 2816
NFC = 22
SEQ = 2048
NS = 128
NREQ = 16
NTOK = SEQ + NS
NMEM = 256
EPS = 1e-6

VEC_NAMES = ["g_ff1_pre", "g_ff1_post", "g_mix_pre", "g_mix_post", "g_mem_pre", "g_mem_kv", "g_mem_post",
             "g_ff2_pre", "g_ff2_post", "lru_conv_b", "lru_b_a", "lru_b_x", "lru_lambda", "conf_conv_b",
             "conf_ln_g", "conf_ln_b"]
VI = {n: i for i, n in enumerate(VEC_NAMES)}
V_W4 = len(VEC_NAMES)
V_W31 = V_W4 + 4
NV = V_W31 + 31


class Sched:
    ENGS = ("pe", "act", "dve", "pool", "sp")

    def __init__(self, nc, stack):
        self.nc = nc
        self.stack = stack
        self.prog = {e: [] for e in self.ENGS}
        self.esem = {e: stack.enter_context(nc.semaphore("es_" + e)) for e in self.ENGS}
        self.ecount = {e: 0 for e in self.ENGS}
        self.waited = {e: {} for e in self.ENGS}
        self.last_write = {}
        self.readers = {}
        self.dma_sems = {}
        self.ps_unread = set()

    def _deps(self, reads, writes):
        deps = []
        for r in reads:
            t = self.last_write.get(r)
            if t is not None:
                deps.append(t)
        for w in writes:
            t = self.last_write.get(w)
            if t is not None:
                deps.append(t)
            deps.extend(self.readers.get(w, {}).values())
        return deps

    def _emit_waits(self, E, deps):
        best = {}
        for (sem, val) in deps:
            k = id(sem)
            if k not in best or best[k][1] < val:
                best[k] = (sem, val)
        for k, (sem, val) in best.items():
            if self.waited[E].get(k, 0) >= val:
                continue
            self.waited[E][k] = val
            self.prog[E].append(("wait", sem, val))

    def _record(self, tok, reads, writes):
        for r in reads:
            d = self.readers.setdefault(r, {})
            k = id(tok[0])
            if k not in d or d[k][1] < tok[1]:
                d[k] = tok
        for w in writes:
            self.last_write[w] = tok
            self.readers[w] = {}

    def op(self, E, fn, reads=(), writes=()):
        psr = [r for r in reads if isinstance(r, tuple) and r[0] == "ps"]
        if E == "pe":
            self.ps_unread.update(w[1] for w in writes if isinstance(w, tuple) and w[0] == "ps")
        else:
            self.ps_unread.difference_update(r[1] for r in psr)
        if psr:
            reads = [r for r in reads if not (isinstance(r, tuple) and r[0] == "ps")]
            writes = list(writes) + psr
        deps = self._deps(reads, writes)
        if E == "pe":
            own = id(self.esem["pe"])
            deps = [d for d in deps if id(d[0]) != own]
        self._emit_waits(E, deps)
        self.ecount[E] += 1
        tok = (self.esem[E], self.ecount[E])
        self.prog[E].append(("op", fn, tok))
        self._record(tok, reads, writes)
        return tok

    def dma(self, Q, out, in_, key, reads=(), writes=()):
        self._emit_waits(Q, self._deps(reads, writes))
        if key not in self.dma_sems:
            self.dma_sems[key] = [self.stack.enter_context(self.nc.semaphore("ds%d" % len(self.dma_sems))), 0]
        ent = self.dma_sems[key]
        ent[1] += 16
        tok = (ent[0], ent[1])

        def fn(eng, out=out, in_=in_):
            return eng.dma_start(out=out, in_=in_)
        self.prog[Q].append(("dma", fn, tok))
        self._record(tok, reads, writes)
        return tok

    def final_wait_all(self, E="sp"):
        toks = list(self.last_write.values())
        for d in self.readers.values():
            toks.extend(d.values())
        self._emit_waits(E, toks)

    def emit(self):
        nc = self.nc
        with nc.Block() as block:
            deco = {"pe": block.tensor, "act": block.scalar, "dve": block.vector, "pool": block.gpsimd, "sp": block.sync}
            for E in self.ENGS:
                items = self.prog[E]

                def body(eng, items=items):
                    for it in items:
                        if it[0] == "wait":
                            eng.wait_ge(it[1], it[2])
                        elif it[0] == "op":
                            it[1](eng).then_inc(it[2][0], 1)
                        else:
                            it[1](eng).then_inc(it[2][0], 16)
                deco[E](body)


def build_program():
    nc = bass.Bass("TRN2", target_bir_lowering=False)

    def din(name, shape, dt=F32):
        return nc.dram_tensor(name, list(shape), dt, kind="ExternalInput").ap()

    def dout(name, shape):
        return nc.dram_tensor(name, list(shape), F32, kind="ExternalOutput").ap()

    x_in = din("x_in", [128, 8, NTOK])
    vecs_d = din("vecs", [128, NV * 8])
    ident_d = din("ident", [128, 128])
    wgu_d = [din("wgu1", [NFC, 128, 2048]), din("wgu2", [NFC, 128, 2048])]
    wd_d = [din("wd1", [8, 128, NFC * 128]), din("wd2", [8, 128, NFC * 128])]
    win_d = din("win", [8, 128, 4096])
    wlru_d = din("wlru", [128, 2048])
    wout_d = din("wout", [8, 128, 2048])
    wq_d = din("wq", [8, 128, 1024])
    wo_d = din("wo", [8, 128, 1024])
    wk_d = din("wk", [8, 128, 1024])
    wv_d = din("wv", [2, 128, 4096])
    memT_d = din("memT", [128, 8 * NMEM])
    zs_d = din("zs_state", [128, 8 * NREQ * 3])
    h0_d = din("h0_state", [128, 8 * NREQ])
    gs_d = din("gs_state", [128, 8 * NREQ * 30])
    gst_d = din("gs_tail", [128, 8 * NREQ * 22])
    kTs_d = din("kTs", [NREQ, 128, 2048])
    vs_d = din("vs", [NREQ, 128, 2048])

    y_out = dout("y_out", [128, 8, NTOK])
    o_lc_p = dout("o_lc_p", [128, 8 * 3])
    o_lh_p = dout("o_lh_p", [128, 8])
    o_cc_p = dout("o_cc_p", [128, 8 * 30])
    o_k_p = dout("o_k_p", [128, 8 * NMEM])
    o_v_p = dout("o_v_p", [128, 2 * D])
    o_lc_s = dout("o_lc_s", [128, 8 * NREQ * 3])
    o_lh_s = dout("o_lh_s", [128, 8 * NREQ])
    o_cc_s_old = dout("o_cc_s_old", [128, 8 * NREQ * 22])
    o_cc_s_new = dout("o_cc_s_new", [128, 8 * NREQ * 8])

    def dscr(name, shape):
        return nc.dram_tensor(name, list(shape), BF16, kind="Internal").ap()

    kTs_s = dscr("kTs_s", [NREQ, 128, 2048])
    vs_s = dscr("vs_s", [NREQ, 128, 2048])

    with ExitStack() as st:
        S = Sched(nc, st)

        def sb(name, shape, dt):
            return nc.alloc_sbuf_tensor(name, list(shape), dt)

        class Seg:
            pass

        def mkseg(sid, N, sample):
            g = Seg()
            g.sid, g.N, g.sample = sid, N, sample
            g.x = sb("x_%s" % sid, [128, 8, N], F32)
            g.hT = sb("hT_%s" % sid, [128, 8, N], BF16)
            g.big = sb("big_%s" % sid, [128, NFC, N], BF16)
            g.yg = sb("yg_%s" % sid, [128, 8, N], F32)
            g.sq = sb("sq_%s" % sid, [128, 8, N], BF16)
            g.cb = sb("cb_%s" % sid, [128, 8, N], BF16)
            g.sd = sb("sd_%s" % sid, [128, N], F32)
            g.rstd = sb("rstd_%s" % sid, [128, N], F32)
            g.TX = sb("TX_%s" % sid, [128, 8, N], F32)
            g.T = [g.TX[:, i, :] for i in range(6)]
            g.xc = [g.TX[:, 6 + i, :] for i in range(2)]
            g.gl = [sb("gl%d_%s" % (i, sid), [128, N], F32) for i in range(2)]
            g.tg = [sb("tg%d_%s" % (i, sid), [128, N], F32) for i in range(2)]
            g.xcb = sb("xcb_%s" % sid, [128, N], BF16)
            return g

        P = mkseg("p", 512, False)
        Sg = mkseg("s", NS, True)

        def R(g, name, *idx):
            return (name, g.sid) + tuple(idx)

        def TXR(g, d):
            return R(g, "T", d) if d < 6 else R(g, "xc", d - 6)

        NRING = 5
        ring = [sb("ring%d" % i, [128, 2048], BF16) for i in range(NRING)]
        vecs = sb("vecs_sb", [128, NV, 8], F32)
        vh = sb("vh", [128, 2, 8], F32)
        cA = sb("cA", [128, 8], F32)
        cAh = sb("cAh", [128, 8], F32)
        tmp8 = sb("tmp8", [128, 8], F32)
        w31h = sb("w31h", [128, 31, 8], F32)
        hb = sb("hb", [128, 2, 8], F32)
        ident32 = sb("ident32", [128, 128], F32)
        identb = sb("identb", [128, 128], BF16)
        ones = sb("ones", [128, 128], BF16)
        epsb = sb("epsb", [128, 1], F32)
        oneb = sb("oneb", [128, 1], F32)
        wlru = sb("wlru_sb", [128, 8, 2, 128], BF16)
        d4 = sb("d4", [128, 4, 128], BF16)
        d31 = sb("d31", [128, 31, 128], BF16)
        kT_p = sb("kT_p", [128, 8, NMEM], BF16)
        v_p = sb("v_p", [128, 2, D], BF16)
        ygf = P.yg[:].rearrange("p c n -> p (c n)")
        mem32 = ygf[:, 0:2048].rearrange("p (c m) -> p c m", c=8)
        stage32 = ygf[:, 2048:4096]
        YGALL = [R(P, "yg", d) for d in range(8)]
        Zx = [sb("Zx%d" % i, [128, 3 + 512], BF16) for i in range(2)]
        Gb = [sb("Gb%d" % i, [128, 30 + 512], BF16) for i in range(2)]
        zx_carry = sb("zx_carry", [128, 8, 3], BF16)
        g_carry = sb("g_carry", [128, 8, 30], BF16)
        h_carry = sb("h_carry", [128, 8], F32)
        lc32 = sb("lc32", [128, 8, 3], F32)
        cc32 = sb("cc32", [128, 8, 30], F32)
        PT = [sb("PT%d" % i, [128, 2, 512], BF16) for i in range(2)]
        Zs = sb("Zs", [128, NREQ, 11], BF16)
        Gs = sb("Gs", [128, NREQ, 38], BF16)
        zs32 = sb("zs32", [128, 8, NREQ, 3], F32)
        h0 = sb("h0", [128, 8, NREQ], F32)
        glus32 = sb("glus32", [128, 8, NREQ, 8], F32)
        gsb = sb("gsb", [128, 8, NREQ, 30], BF16)
        lcs32 = sb("lcs32", [128, 8, NREQ, 3], F32)
        lhs32 = sb("lhs32", [128, 8, NREQ], F32)
        tmp16 = sb("tmp16", [128, NREQ], F32)
        PTs = [sb("PTs%d" % i, [128, 64], BF16) for i in range(2)]

        ps = [nc.alloc_psum_tensor("ps%d" % i, [128, 512], F32) for i in range(8)]
        state = {"bank": 0, "pool": list(range(8)), "ring": 0, "held": set(), "si": 0}

        def newbank():
            while True:
                b = state["pool"][state["bank"] % len(state["pool"])]
                state["bank"] += 1
                if b not in state["held"]:
                    assert b not in S.ps_unread, "PSUM bank %d re-allocated before its consumer was emitted" % b
                    return b

        def load_slab(src2d, width, scr=None, skey=None):
            assert width <= 2048
            i = state["ring"] % NRING
            state["ring"] += 1
            if scr is None:
                S.dma("pool", ring[i][:, :width], src2d, key=("ring", i), reads=state.pop("first_dep", []), writes=[("ring", i)])
            elif state["si"] == 0:
                S.dma("pool", ring[i][:, :width], src2d, key=("ring", i), writes=[("ring", i)])
                S.dma("sp", scr, ring[i][:, :width], key=("scrst", i), reads=[("ring", i)], writes=[("scr", skey)])
            else:
                extra = ["kvcast_all"] if skey[0] in ("k", "v") else []
                S.dma("pool", ring[i][:, :width], scr, key=("ring", i), reads=[("scr", skey)] + extra, writes=[("ring", i)])
            return ring[i], ("ring", i)

        def V(name, c):
            return vecs[:, VI[name], c:c + 1]

        warm_t = sb("warm_t", [128, 2], F32)
        S.op("dve", lambda e: e.memset(warm_t[:], 1.0), writes=["warm"])

        def act_warm(func):
            S.op("act", lambda e: e.activation(out=warm_t[:, 1:2], in_=warm_t[:, 0:1], func=func), reads=["warm"], writes=["warm_o"])

        def r8(ap):
            return ap.rearrange("p (r t) -> p r t", t=8)

        def mm_group(bank, N, pairs, reads, out=None, fine=None, first=True, last=True):
            o = ps[bank][:, :N] if out is None else out
            if fine is not None:
                n = len(pairs)
                tok = None
                for k, (l, r) in enumerate(pairs):
                    tok = S.op("pe", lambda pe, l=l, r=r, k=k: pe.matmul(o, l, r, start=(first and k == 0), stop=(last and k == n - 1)),
                               reads=list(reads) + list(fine[k]), writes=[("ps", bank)])
                return tok

            def fn(pe, pairs=pairs, o=o):
                n = len(pairs)
                ins = None
                for k, (l, r) in enumerate(pairs):
                    ins = pe.matmul(o, l, r, start=(first and k == 0), stop=(last and k == n - 1))
                return ins
            return S.op("pe", fn, reads=reads, writes=[("ps", bank)])

        def stats_to_rstd(g, scale):
            N = g.N
            b = newbank()
            act_warm(AF.Ln)
            for d in range(8):
                S.op("pe", lambda pe, d=d, b=b: pe.matmul(ps[b][:, :N], ones[:, :], g.sq[:, d, :N], start=(d == 0), stop=(d == 7)),
                     reads=[R(g, "sq", d), "ones"], writes=[("ps", b)])
            S.op("act", lambda e: e.activation(out=g.sd[:, :N], in_=ps[b][:, :N], func=AF.Ln, bias=epsb[:, :], scale=scale),
                 reads=[("ps", b), "epsb"], writes=[R(g, "sd")])
            S.op("act", lambda e: e.activation(out=g.rstd[:, :N], in_=g.sd[:, :N], func=AF.Exp, scale=-0.5), reads=[R(g, "sd")], writes=[R(g, "rstd")])

        def prenorm(g, gname, xb=None, N=None, xregs=None, xr=None):
            xb = g.x if xb is None else xb
            N = g.N if N is None else N
            if xr is None:
                xr = (lambda d: xregs) if xregs is not None else (lambda d: [R(g, "x", d)])
            for d in range(8):
                S.op("act", lambda e, d=d: e.activation(out=g.sq[:, d, :N], in_=xb[:, d, :N], func=AF.Square), reads=xr(d), writes=[R(g, "sq", d)])
            b = newbank()
            for d in range(8):
                S.op("pe", lambda pe, d=d, b=b: pe.matmul(ps[b][:, :N], ones[:, :], g.sq[:, d, :N], start=(d == 0), stop=(d == 7)),
                     reads=[R(g, "sq", d), "ones"], writes=[("ps", b)])
            S.op("act", lambda e: e.activation(out=g.sd[:, :N], in_=ps[b][:, :N], func=AF.Ln, bias=epsb[:, :], scale=1.0 / D),
                 reads=[("ps", b), "epsb"], writes=[R(g, "sd")])
            S.op("act", lambda e: e.activation(out=g.rstd[:, :N], in_=g.sd[:, :N], func=AF.Exp, scale=-0.5), reads=[R(g, "sd")], writes=[R(g, "rstd")])
            for d in range(8):
                S.op("dve", lambda e, d=d: e.scalar_tensor_tensor(out=g.hT[:, d, :N], in0=xb[:, d, :N], scalar=V(gname, d),
                                                                   in1=g.rstd[:, :N], op0=ALU.mult, op1=ALU.mult),
                     reads=xr(d) + [R(g, "rstd"), "vecs"], writes=[R(g, "h", d)])

        def postnorm(segs, gap, producer, bank_of=None, xsrc=None, on_x_done=None, after_d=None):
            for d in range(8):
                prod = producer(d)
                for g in segs:
                    N = g.N
                    b = newbank() if bank_of is None else bank_of(g, d)
                    prod(g, b)
                    S.op("act", lambda e, d=d, b=b, g=g, N=N: e.activation(out=g.sq[:, d, :N], in_=ps[b][:, :N], func=AF.Square),
                         reads=[("ps", b)], writes=[R(g, "sq", d)])
                    S.op("dve", lambda e, d=d, b=b, g=g, N=N: e.tensor_scalar(out=g.yg[:, d, :N], in0=ps[b][:, :N], scalar1=gap(d), scalar2=None,
                                                                             op0=ALU.mult),
                         reads=[("ps", b), "vecs"], writes=[R(g, "yg", d)])
                if after_d is not None:
                    after_d(d)
            for g in segs:
                stats_to_rstd(g, 1.0 / D)
            for g in segs:
                N = g.N
                for d in range(8):
                    S.op("dve", lambda e, d=d, g=g, N=N: e.tensor_tensor(out=g.yg[:, d, :N], in0=g.yg[:, d, :N], in1=g.rstd[:, :N], op=ALU.mult),
                         reads=[R(g, "yg", d), R(g, "rstd")], writes=[R(g, "yg", d)])
                    if xsrc is not None and xsrc(g, d) is not None:
                        xin, xreg = xsrc(g, d)
                    else:
                        xin, xreg = g.x[:, d, :N], R(g, "x", d)
                    S.op("dve", lambda e, d=d, g=g, N=N, xin=xin: e.tensor_tensor(out=g.x[:, d, :N], in0=xin, in1=g.yg[:, d, :N], op=ALU.add),
                         reads=[R(g, "yg", d), xreg], writes=[R(g, "x", d)])
                    if on_x_done is not None:
                        on_x_done(g, d)

        def ffn(segs, li, on_x_done=None, mid_hook=None, after_d=None):
            pre = "g_ff1_pre" if li == 0 else "g_ff2_pre"
            for g in segs:
                if li == 0 and not g.sample and state.pop("early_pn", False):
                    continue
                if li == 0 and g.sample and state.pop("early_pn_s", False):
                    continue
                if li == 0 and not g.sample:
                    prenorm(g, pre, xb=g.TX, xr=lambda d, g=g: [TXR(g, d)])
                else:
                    prenorm(g, pre)
            xsrc = None
            if li == 0:
                xsrc = lambda g, d: None if g.sample else (g.TX[:, d, :g.N], TXR(g, d))
            for j in range(NFC):
                slab, rk = load_slab(wgu_d[li][j], 2048)
                sv = slab[:, :2048].rearrange("p (s k f) -> p s k f", s=2, k=8)
                for g in segs:
                    N = g.N
                    hreads = [R(g, "h", d) for d in range(8)]
                    bg, bu = newbank(), newbank()
                    if j == 0:
                        mm_group(bg, N, [(sv[:, 0, k, :], g.hT[:, k, :N]) for k in range(8)], reads=[rk], fine=[[R(g, "h", k)] for k in range(8)])
                    else:
                        mm_group(bg, N, [(sv[:, 0, k, :], g.hT[:, k, :N]) for k in range(8)], reads=hreads + [rk])
                    mm_group(bu, N, [(sv[:, 1, k, :], g.hT[:, k, :N]) for k in range(8)], reads=hreads + [rk])
                    sg = g.tg[j % 2]
                    S.op("act", lambda e, bg=bg, sg=sg, N=N: e.activation(out=sg[:, :N], in_=ps[bg][:, :N], func=AF.Silu),
                         reads=[("ps", bg)], writes=[R(g, "tg", j % 2)])
                    S.op("dve", lambda e, bu=bu, sg=sg, j=j, g=g, N=N: e.tensor_tensor(out=g.big[:, j, :N], in0=sg[:, :N], in1=ps[bu][:, :N], op=ALU.mult),
                         reads=[("ps", bu), R(g, "tg", j % 2)], writes=[R(g, "big", j)])

            if mid_hook is not None:
                mid_hook()

            def producer(d):
                slabs = {}
                slab, rk = load_slab(wd_d[li][d][:, 0:2048], 2048)
                sva = slab[:, :2048].rearrange("p (k f) -> p k f", k=16)
                slab2, rk2 = load_slab(wd_d[li][d][:, 2048:2816], 768)
                svb = slab2[:, :768].rearrange("p (k f) -> p k f", k=6)
                wsl = [sva[:, k, :] for k in range(16)] + [svb[:, k, :] for k in range(6)]

                def fn(g, b):
                    N = g.N
                    pairs = [(wsl[k], g.big[:, k, :N]) for k in range(NFC)]
                    if d == 0:
                        mm_group(b, N, pairs, reads=[rk, rk2], fine=[[R(g, "big", k)] for k in range(NFC)])
                    else:
                        mm_group(b, N, pairs, reads=[R(g, "big", k) for k in range(NFC)] + [rk, rk2])
                return fn
            postnorm(segs, lambda d: vh[:, li, d:d + 1], producer, xsrc=xsrc, on_x_done=on_x_done, after_d=after_d)

        def early_prenorm_parts(g, gname):
            N = g.N

            def part_squares():
                for d in range(8):
                    S.op("act", lambda e, d=d: e.activation(out=g.cb[:, d, :N], in_=g.TX[:, d, :N], func=AF.Square),
                         reads=[TXR(g, d)], writes=[R(g, "cb", d)])

            def part_rest():
                b = newbank()
                for d in range(8):
                    S.op("pe", lambda pe, d=d, b=b: pe.matmul(ps[b][:, :N], ones[:, :], g.cb[:, d, :N], start=(d == 0), stop=(d == 7)),
                         reads=[R(g, "cb", d), "ones"], writes=[("ps", b)])
                S.op("act", lambda e: e.activation(out=g.sd[:, :N], in_=ps[b][:, :N], func=AF.Ln, bias=epsb[:, :], scale=1.0 / D),
                     reads=[("ps", b), "epsb"], writes=[R(g, "sd")])
                S.op("act", lambda e: e.activation(out=g.rstd[:, :N], in_=g.sd[:, :N], func=AF.Exp, scale=-0.5), reads=[R(g, "sd")], writes=[R(g, "rstd")])
                for d in range(8):
                    S.op("dve", lambda e, d=d: e.scalar_tensor_tensor(out=g.hT[:, d, :N], in0=g.TX[:, d, :N], scalar=V(gname, d),
                                                                       in1=g.rstd[:, :N], op0=ALU.mult, op1=ALU.mult),
                         reads=[TXR(g, d), R(g, "rstd"), "vecs"], writes=[R(g, "h", d)])
            return part_squares, part_rest

        TX_ALL = [TXR(P, d) for d in range(8)]
        S.dma("sp", vecs[:].rearrange("p v c -> p (v c)"), vecs_d, key="vecs", writes=["vecs"])
        S.dma("sp", P.TX[:, 0:4, :], x_in[:, 0:4, 0:512], key="xin_p", writes=TX_ALL[0:4])
        S.dma("act", P.TX[:, 4:8, :], x_in[:, 4:8, 0:512], key="xin_p2", writes=TX_ALL[4:8])
        state["first_dep"] = list(TX_ALL)
        S.dma("sp", ident32[:], ident_d, key="ident", writes=["ident32"])
        S.op("dve", lambda e: e.memset(epsb[:], EPS), writes=["epsb"])
        S.op("dve", lambda e: e.memset(oneb[:], 1.0), writes=["oneb"])
        S.op("dve", lambda e: e.memset(ones[:], 1.0), writes=["ones"])
        S.op("dve", lambda e: e.tensor_scalar(out=w31h[:], in0=vecs[:, V_W31:V_W31 + 31, :], scalar1=0.5, scalar2=None, op0=ALU.mult), reads=["vecs"], writes=["w31h"])
        S.op("dve", lambda e: e.tensor_scalar(out=hb[:, 0, :], in0=vecs[:, VI["lru_b_a"], :], scalar1=0.5, scalar2=None, op0=ALU.mult), reads=["vecs"], writes=["hb"])
        S.op("dve", lambda e: e.tensor_scalar(out=hb[:, 1, :], in0=vecs[:, VI["lru_b_x"], :], scalar1=0.5, scalar2=None, op0=ALU.mult), reads=["vecs"], writes=["hb"])
        S.op("dve", lambda e: e.tensor_copy(out=identb[:], in_=ident32[:]), reads=["ident32"], writes=["identb"])
        S.op("dve", lambda e: e.memset(zx_carry[:], 0.0), writes=["zx_carry"])
        S.op("dve", lambda e: e.memset(g_carry[:], 0.0), writes=["g_carry"])
        S.op("dve", lambda e: e.memset(h_carry[:], 0.0), writes=["h_carry"])
        for li, nm in enumerate(["g_ff1_post", "g_ff2_post"]):
            S.op("dve", lambda e, li=li, nm=nm: e.tensor_scalar(out=vh[:, li, :], in0=vecs[:, VI[nm], :], scalar1=0.5, scalar2=None, op0=ALU.mult),
                 reads=["vecs"], writes=["vecs"])
        S.op("act", lambda e: e.activation(out=tmp8[:], in_=vecs[:, VI["lru_lambda"], :], func=AF.Exp, scale=-1.0), reads=["vecs"], writes=["tmp8"])
        S.op("act", lambda e: e.activation(out=tmp8[:], in_=tmp8[:], func=AF.Ln, bias=oneb[:, :], scale=1.0), reads=["tmp8", "oneb"], writes=["tmp8"])
        S.op("dve", lambda e: e.tensor_scalar(out=cA[:], in0=tmp8[:], scalar1=-8.0, scalar2=None, op0=ALU.mult), reads=["tmp8"], writes=["cA"])
        S.op("dve", lambda e: e.tensor_scalar(out=cAh[:], in0=tmp8[:], scalar1=-4.0, scalar2=None, op0=ALU.mult), reads=["tmp8"], writes=["cA"])
        S.dma("sp", zs32[:].rearrange("p c r t -> p (c r t)"), zs_d, key="zs32", writes=["zs32"])
        S.dma("sp", h0[:].rearrange("p c r -> p (c r)"), h0_d, key="h0", writes=["h0"])
        xsf = Sg.x[:].rearrange("p c n -> p (c n)")
        XS_ALL = [R(Sg, "x", d) for d in range(8)]
        def gs_tail_bounce():
            for part in range(3):
                w0, w1 = part * 1024, min((part + 1) * 1024, 8 * NREQ * 22)
                S.dma("sp", xsf[:, 0:w1 - w0], gst_d[:, w0:w1], key="gst_in", writes=XS_ALL)
                S.dma("sp", o_cc_s_old[:, w0:w1], xsf[:, 0:w1 - w0], key="gst_out", reads=XS_ALL)

        def mem_kv():
            S.dma("sp", ygf[:, 0:2048], memT_d, key="mem", writes=YGALL)
            prenorm(P, "g_mem_kv", xb=mem32, N=NMEM, xregs=YGALL)
            mreads = [R(P, "h", d) for d in range(8)]
            for o in range(8):
                slab, rk = load_slab(wk_d[o], 1024)
                sv = slab[:, :1024].rearrange("p (k f) -> p k f", k=8)
                b = newbank()
                mm_group(b, NMEM, [(sv[:, k, :], P.hT[:, k, :NMEM]) for k in range(8)], reads=mreads + [rk])
                S.op("act", lambda e, o=o, b=b: e.activation(out=stage32[:, o * NMEM:(o + 1) * NMEM], in_=ps[b][:, :NMEM], func=AF.Copy),
                     reads=[("ps", b)], writes=YGALL)
                S.op("dve", lambda e, o=o, b=b: e.tensor_copy(out=kT_p[:, o, :], in_=ps[b][:, :NMEM]), reads=[("ps", b)], writes=["kT_p"])
            S.dma("sp", o_k_p, stage32[:, :], key="o_k_p", reads=YGALL)
            for half in range(2):
                svs = []
                for q in range(2):
                    slab, rk = load_slab(wv_d[half][:, q * 2048:(q + 1) * 2048], 2048)
                    svs.append((slab[:, :2048].rearrange("p (k f) -> p k f", k=4), rk))
                for mc in range(2):
                    b = newbank()
                    mm_group(b, 512, [(P.hT[:, k, mc * 128:(mc + 1) * 128], svs[k // 4][0][:, k % 4, :]) for k in range(8)],
                             reads=mreads + [svs[0][1], svs[1][1]])
                    S.op("act", lambda e, b=b, mc=mc, half=half: e.activation(out=stage32[:, mc * 1024 + half * 512: mc * 1024 + half * 512 + 512],
                                                                            in_=ps[b][:, :], func=AF.Copy),
                         reads=[("ps", b)], writes=YGALL)
                    S.op("dve", lambda e, b=b, mc=mc, half=half: e.tensor_copy(out=v_p[:, mc, half * 512:(half + 1) * 512], in_=ps[b][:, :]),
                         reads=[("ps", b)], writes=["v_p"])
            S.dma("sp", o_v_p, stage32[:, :], key="o_v_p", reads=YGALL)

        def mix_early(g, c, svA, rkA, svB, rkB, out):
            N, sample = g.N, g.sample
            hreads = [R(g, "h", d) for d in range(8)]
            zx, G = Zx[c % 2], Gb[c % 2]
            tg, gl, xc32 = g.tg[0], g.gl[c % 2], g.xc[c % 2]
            b0, b4, b5, b6 = newbank(), newbank(), newbank(), newbank()
            if c == 0:
                mm_group(b6, N, [(svB[:, 1, k, :], g.hT[:, k, :N]) for k in range(8)], reads=[rkB], fine=[[R(g, "h", k)] for k in range(8)])
            else:
                mm_group(b6, N, [(svB[:, 1, k, :], g.hT[:, k, :N]) for k in range(8)], reads=hreads + [rkB])
            mm_group(b5, N, [(svB[:, 0, k, :], g.hT[:, k, :N]) for k in range(8)], reads=hreads + [rkB])
            mm_group(b0, N, [(svA[:, 0, k, :], g.hT[:, k, :N]) for k in range(8)], reads=hreads + [rkA])
            mm_group(b4, N, [(svA[:, 1, k, :], g.hT[:, k, :N]) for k in range(8)], reads=hreads + [rkA])
            if not sample:
                S.op("dve", lambda e: e.tensor_copy(out=zx[:, 0:3], in_=zx_carry[:, c, :]), reads=["zx_carry"], writes=[("zx", c % 2)])
            else:
                S.op("dve", lambda e: e.tensor_copy(out=Zs[:, :, 0:3], in_=zs32[:, c, :, :]), reads=["zs32"], writes=["Zs"])
            yield
            S.op("act", lambda e: e.activation(out=tg[:, :N], in_=ps[b6][:, :N], func=AF.Tanh, scale=0.5), reads=[("ps", b6)], writes=[R(g, "tg", 0)])
            if not sample:
                S.op("dve", lambda e: e.tensor_copy(out=G[:, 0:30], in_=g_carry[:, c, :]), reads=["g_carry"], writes=[("G", c % 2)])
                S.op("dve", lambda e: e.scalar_tensor_tensor(out=G[:, 30:30 + N], in0=tg[:, :N], scalar=1.0, in1=ps[b5][:, :N], op0=ALU.add, op1=ALU.mult),
                     reads=[R(g, "tg", 0), ("ps", b5)], writes=[("G", c % 2)])
                S.op("dve", lambda e: e.scalar_tensor_tensor(out=cc32[:, c, :], in0=tg[:, N - 30:N], scalar=1.0, in1=ps[b5][:, N - 30:N],
                                                             op0=ALU.add, op1=ALU.mult),
                     reads=[R(g, "tg", 0), ("ps", b5)], writes=["cc32"])
                S.op("dve", lambda e: e.tensor_scalar(out=cc32[:, c, :], in0=cc32[:, c, :], scalar1=0.5, scalar2=None, op0=ALU.mult),
                     reads=["cc32"], writes=["cc32"])
                S.op("dve", lambda e: e.tensor_copy(out=g_carry[:, c, :], in_=G[:, N:N + 30]), reads=[("G", c % 2)], writes=["g_carry"])
                conv_rhs = [G[:, k:k + N] for k in range(31)]
                g_reads = [("G", c % 2)]
            else:
                tgv, p5v = r8(tg[:, :N]), r8(ps[b5][:, :N])
                S.op("dve", lambda e: e.tensor_scalar(out=Gs[:, :, 0:30], in0=gsb[:, c, :, :], scalar1=2.0, scalar2=None, op0=ALU.mult),
                     reads=["gsb"], writes=["Gs"])
                S.op("dve", lambda e: e.scalar_tensor_tensor(out=Gs[:, :, 30:38], in0=tgv, scalar=1.0, in1=p5v, op0=ALU.add, op1=ALU.mult),
                     reads=[R(g, "tg", 0), ("ps", b5)], writes=["Gs"])
                S.op("dve", lambda e: e.scalar_tensor_tensor(out=glus32[:, c, :, :], in0=tgv, scalar=1.0, in1=p5v, op0=ALU.add, op1=ALU.mult),
                     reads=[R(g, "tg", 0), ("ps", b5)], writes=["glus32"])
                S.op("dve", lambda e: e.tensor_scalar(out=glus32[:, c, :, :], in0=glus32[:, c, :, :], scalar1=0.5, scalar2=None, op0=ALU.mult),
                     reads=["glus32"], writes=["glus32"])
                conv_rhs = [Gs[:, :, k:k + 8] for k in range(31)]
                g_reads = ["Gs"]
            if not sample:
                S.op("act", lambda e: e.activation(out=zx[:, 3:3 + N], in_=ps[b0][:, :N], func=AF.Copy), reads=[("ps", b0)], writes=[("zx", c % 2)])
                S.op("dve", lambda e: e.tensor_copy(out=lc32[:, c, :], in_=ps[b0][:, N - 3:N]), reads=[("ps", b0)], writes=["lc32"])
                S.op("dve", lambda e: e.tensor_copy(out=zx_carry[:, c, :], in_=zx[:, N:N + 3]), reads=[("zx", c % 2)], writes=["zx_carry"])
                conv4_rhs = [zx[:, k:k + N] for k in range(4)]
                cv_reads = [("zx", c % 2)]
            else:
                psv = r8(ps[b0][:, :N])
                S.op("act", lambda e: e.activation(out=Zs[:, :, 3:11], in_=psv, func=AF.Copy), reads=[("ps", b0)], writes=["Zs"])
                S.op("dve", lambda e: e.tensor_copy(out=lcs32[:, c, :, :], in_=psv[:, :, 5:8]), reads=[("ps", b0)], writes=["lcs32"])
                conv4_rhs = [Zs[:, :, k:k + 8] for k in range(4)]
                cv_reads = ["Zs"]
            S.op("act", lambda e: e.activation(out=gl[:, :N], in_=ps[b4][:, :N], func=AF.Gelu_apprx_tanh), reads=[("ps", b4)], writes=[R(g, "gl", c % 2)])
            yield
            b1 = newbank()
            o1 = r8(ps[b1][:, :N]) if sample else None
            mm_group(b1, N, [(d4[:, k, :], conv4_rhs[k]) for k in range(4)], reads=cv_reads + ["d4"], out=o1)
            S.op("act", lambda e: e.activation(out=xc32[:, :N], in_=ps[b1][:, :N], func=AF.Identity, bias=V("lru_conv_b", c), scale=1.0),
                 reads=[("ps", b1), "vecs"], writes=[R(g, "xc", c % 2)])
            S.op("dve", lambda e: e.tensor_copy(out=g.xcb[:, :N], in_=xc32[:, :N]), reads=[R(g, "xc", c % 2)], writes=[R(g, "xcb")])
            yield
            b7 = newbank()
            o7 = r8(ps[b7][:, :N]) if sample else None
            mm_group(b7, N, [(d31[:, k, :], conv_rhs[k]) for k in range(31)], reads=g_reads + ["d31"], out=o7)
            S.op("act", lambda e: e.activation(out=g.yg[:, c, :N], in_=ps[b7][:, :N], func=AF.Identity, bias=V("conf_conv_b", c), scale=1.0),
                 reads=[("ps", b7), "vecs"], writes=[R(g, "yg", c)])
            S.op("act", lambda e: e.activation(out=g.sq[:, c, :N], in_=g.yg[:, c, :N], func=AF.Square), reads=[R(g, "yg", c)], writes=[R(g, "sq", c)])
            S.op("act", lambda e: e.activation(out=g.cb[:, c, :N], in_=g.yg[:, c, :N], func=AF.Copy), reads=[R(g, "yg", c)], writes=[R(g, "cb", c)])
            yield
            b2, b3 = newbank(), newbank()
            mm_group(b2, N, [(wlru[:, c, 0, :], g.xcb[:, :N])], reads=[R(g, "xcb"), "wlru"])
            mm_group(b3, N, [(wlru[:, c, 1, :], g.xcb[:, :N])], reads=[R(g, "xcb"), "wlru"])
            state["held"].update((b2, b3))
            out[g.sid] = (b2, b3)

        def mix_tail_act(g, c, banks):
            N = g.N
            b2, b3 = banks
            r32, i32, a32, s32 = g.T[1], g.T[2], g.T[3], g.T[4]
            S.op("act", lambda e: e.activation(out=r32[:, :N], in_=ps[b2][:, :N], func=AF.Tanh, bias=hb[:, 0, c:c + 1], scale=0.5),
                 reads=[("ps", b2), "hb"], writes=[R(g, "T", 1)])
            S.op("act", lambda e: e.activation(out=i32[:, :N], in_=ps[b3][:, :N], func=AF.Tanh, bias=hb[:, 1, c:c + 1], scale=0.5),
                 reads=[("ps", b3), "hb"], writes=[R(g, "T", 2)])
            state["held"].difference_update((b2, b3))
            yield
            S.op("act", lambda e: e.activation(out=a32[:, :N], in_=r32[:, :N], func=AF.Exp, bias=cAh[:, c:c + 1], scale=cAh[:, c:c + 1]),
                 reads=[R(g, "T", 1), "cA"], writes=[R(g, "T", 3)])
            S.op("act", lambda e: e.activation(out=s32[:, :N], in_=r32[:, :N], func=AF.Exp, bias=cA[:, c:c + 1], scale=cA[:, c:c + 1]),
                 reads=[R(g, "T", 1), "cA"], writes=[R(g, "T", 4)])
            yield
            S.op("act", lambda e: e.activation(out=s32[:, :N], in_=s32[:, :N], func=AF.Ln, bias=oneb[:, :], scale=-1.0),
                 reads=[R(g, "T", 4), "oneb"], writes=[R(g, "T", 4)])
            yield
            S.op("act", lambda e: e.activation(out=s32[:, :N], in_=s32[:, :N], func=AF.Exp, scale=0.5), reads=[R(g, "T", 4)], writes=[R(g, "T", 4)])

        def mix_tail_dve(g, c):
            N, sample = g.N, g.sample
            i32, a32, s32, hs32 = g.T[2], g.T[3], g.T[4], g.T[5]
            u32 = i32
            xc32, gl = g.xc[c % 2], g.gl[c % 2]
            S.op("dve", lambda e: e.scalar_tensor_tensor(out=u32[:, :N], in0=i32[:, :N], scalar=1.0, in1=xc32[:, :N], op0=ALU.add, op1=ALU.mult),
                 reads=[R(g, "T", 2), R(g, "xc", c % 2)], writes=[R(g, "T", 2)])
            S.op("dve", lambda e: e.scalar_tensor_tensor(out=u32[:, :N], in0=u32[:, :N], scalar=0.5, in1=s32[:, :N], op0=ALU.mult, op1=ALU.mult),
                 reads=[R(g, "T", 2), R(g, "T", 4)], writes=[R(g, "T", 2)])
            if not sample:
                S.op("dve", lambda e: e.tensor_tensor_scan(out=hs32[:, :N], data0=a32[:, :N], data1=u32[:, :N], initial=h_carry[:, c:c + 1],
                                                          op0=ALU.mult, op1=ALU.add),
                     reads=[R(g, "T", 3), R(g, "T", 2), "h_carry"], writes=[R(g, "T", 5)])
                S.op("dve", lambda e: e.tensor_copy(out=h_carry[:, c:c + 1], in_=hs32[:, N - 1:N]), reads=[R(g, "T", 5)], writes=["h_carry"])
            else:
                av, uv = r8(a32[:, :N]), r8(u32[:, :N])
                S.op("dve", lambda e: e.tensor_tensor(out=tmp16[:, :], in0=av[:, :, 0], in1=h0[:, c, :], op=ALU.mult),
                     reads=[R(g, "T", 3), "h0"], writes=["tmp16"])
                S.op("dve", lambda e: e.tensor_tensor(out=uv[:, :, 0], in0=uv[:, :, 0], in1=tmp16[:, :], op=ALU.add),
                     reads=[R(g, "T", 2), "tmp16"], writes=[R(g, "T", 2)])
                S.op("dve", lambda e: e.memset(av[:, :, 0], 0.0), reads=["tmp16"], writes=[R(g, "T", 3)])
                S.op("dve", lambda e: e.tensor_tensor_scan(out=hs32[:, :N], data0=a32[:, :N], data1=u32[:, :N], initial=0.0,
                                                          op0=ALU.mult, op1=ALU.add),
                     reads=[R(g, "T", 3), R(g, "T", 2)], writes=[R(g, "T", 5)])
                hv = r8(hs32[:, :N])
                S.op("dve", lambda e: e.tensor_copy(out=lhs32[:, c, :], in_=hv[:, :, 7]), reads=[R(g, "T", 5)], writes=["lhs32"])
            S.op("dve", lambda e: e.tensor_tensor(out=g.big[:, c, :N], in0=gl[:, :N], in1=hs32[:, :N], op=ALU.mult),
                 reads=[R(g, "gl", c % 2), R(g, "T", 5)], writes=[R(g, "big", c)])

        def mixer(segs):
            for g in segs:
                prenorm(g, "g_mix_pre")
            pend = None

            def run_staged(gens, nstages):
                for _stage in range(nstages):
                    for gen in gens:
                        next(gen, None)

            for c in range(9):
                newp = {}
                gens = []
                if c < 8:
                    slabB, rkB = load_slab(win_d[c][:, 2048:4096], 2048)
                    slabA, rkA = load_slab(win_d[c][:, 0:2048], 2048)
                    svA = slabA[:, :2048].rearrange("p (s k f) -> p s k f", s=2, k=8)
                    svB = slabB[:, :2048].rearrange("p (s k f) -> p s k f", s=2, k=8)
                    S.op("dve", lambda e, c=c: e.tensor_tensor(out=d4[:, :, :], in0=identb[:].unsqueeze(1).broadcast_to([128, 4, 128]),
                                                              in1=vecs[:, V_W4:V_W4 + 4, c:c + 1].broadcast_to([128, 4, 128]), op=ALU.mult),
                         reads=["identb", "vecs"], writes=["d4"])
                    if state["si"] in (1, 2):
                        for r in ((state["si"] - 1) * 8 + c,):
                            S.dma("pool", kTs_s[r], kTs_d[r], key="kvcast", writes=[("scr", ("k", r))])
                            S.dma("pool", vs_s[r], vs_d[r], key="kvcast", writes=[("scr", ("v", r))] + (["kvcast_all"] if r == NREQ - 1 else []))
                    gens = [mix_early(g, c, svA, rkA, svB, rkB, newp) for g in segs]
                    for gi, gen in enumerate(gens):
                        next(gen, None)
                        if gi == 0:
                            S.op("dve", lambda e, c=c: e.tensor_tensor(out=d31[:, :, :], in0=identb[:].unsqueeze(1).broadcast_to([128, 31, 128]),
                                                                      in1=w31h[:, :, c:c + 1].broadcast_to([128, 31, 128]), op=ALU.mult),
                                 reads=["identb", "w31h"], writes=["d31"])
                        next(gen, None)
                if pend is not None:
                    run_staged([mix_tail_act(g, c - 1, pend[g.sid]) for g in segs], 4)
                if c < 8:
                    run_staged(gens, 3)
                    if pend is not None and c < 7:
                        act_warm(AF.Gelu_apprx_tanh)
                if pend is not None:
                    for g in segs:
                        mix_tail_dve(g, c - 1)
                pend = newp if c < 8 else None
            def ln_chain(g):
                N = g.N
                mean32, m2 = g.T[0], g.T[1]
                t1 = [g.T[2], g.T[3]]
                bm, bq = newbank(), newbank()
                mm_group(bm, N, [(ones[:, :], g.cb[:, d, :N]) for d in range(8)], reads=[R(g, "cb", d) for d in range(8)] + ["ones"])
                mm_group(bq, N, [(ones[:, :], g.sq[:, d, :N]) for d in range(8)], reads=[R(g, "sq", d) for d in range(8)] + ["ones"])
                yield
                S.op("act", lambda e: e.activation(out=mean32[:, :N], in_=ps[bm][:, :N], func=AF.Copy, scale=1.0 / D),
                     reads=[("ps", bm)], writes=[R(g, "T", 0)])
                S.op("dve", lambda e: e.tensor_tensor(out=m2[:, :N], in0=mean32[:, :N], in1=mean32[:, :N], op=ALU.mult),
                     reads=[R(g, "T", 0)], writes=[R(g, "T", 1)])
                S.op("dve", lambda e: e.scalar_tensor_tensor(out=m2[:, :N], in0=ps[bq][:, :N], scalar=1.0 / D, in1=m2[:, :N],
                                                             op0=ALU.mult, op1=ALU.subtract),
                     reads=[("ps", bq), R(g, "T", 1)], writes=[R(g, "T", 1)])
                yield
                S.op("act", lambda e: e.activation(out=g.sd[:, :N], in_=m2[:, :N], func=AF.Ln, bias=epsb[:, :], scale=1.0),
                     reads=[R(g, "T", 1), "epsb"], writes=[R(g, "sd")])
                yield
                S.op("act", lambda e: e.activation(out=g.rstd[:, :N], in_=g.sd[:, :N], func=AF.Exp, scale=-0.5), reads=[R(g, "sd")], writes=[R(g, "rstd")])
                yield
                for d in range(8):
                    tt = t1[d % 2]
                    S.op("dve", lambda e, d=d, tt=tt: e.tensor_tensor(out=tt[:, :N], in0=g.yg[:, d, :N], in1=mean32[:, :N], op=ALU.subtract),
                         reads=[R(g, "yg", d), R(g, "T", 0)], writes=[R(g, "T", 2 + d % 2)])
                    S.op("dve", lambda e, tt=tt, d=d: e.tensor_tensor(out=tt[:, :N], in0=tt[:, :N], in1=g.rstd[:, :N], op=ALU.mult),
                         reads=[R(g, "T", 2 + d % 2), R(g, "rstd")], writes=[R(g, "T", 2 + d % 2)])
                    S.op("act", lambda e, tt=tt, d=d: e.activation(out=g.big[:, 8 + d, :N], in_=tt[:, :N], func=AF.Silu,
                                                                  bias=V("conf_ln_b", d), scale=V("conf_ln_g", d)),
                         reads=[R(g, "T", 2 + d % 2), "vecs"], writes=[R(g, "big", 8 + d)])
                    yield

            act_warm(AF.Ln)
            run_staged([ln_chain(g) for g in segs], 12)
            if True:
                g = segs[0]
                N = g.N
                obank = {}
                for d in range(8):
                    slab, rk = load_slab(wout_d[d][:, 0:1024], 1024)
                    sv = slab[:, :1024].rearrange("p (k f) -> p k f", k=8)
                    obank[d] = newbank()
                    mm_group(obank[d], N, [(sv[:, k, :], g.big[:, k, :N]) for k in range(8)], reads=[R(g, "big", k) for k in range(8)] + [rk],
                             first=True, last=False)

                def producer(d):
                    slab, rk = load_slab(wout_d[d][:, 1024:2048], 1024)
                    sv = slab[:, :1024].rearrange("p (k f) -> p k f", k=8)

                    def fn(g, b):
                        if d == 0:
                            mm_group(b, g.N, [(sv[:, k, :], g.big[:, 8 + k, :g.N]) for k in range(8)], reads=[rk],
                                     fine=[[R(g, "big", 8 + k)] for k in range(8)], first=False, last=True)
                        else:
                            mm_group(b, g.N, [(sv[:, k, :], g.big[:, 8 + k, :g.N]) for k in range(8)], reads=[R(g, "big", 8 + k) for k in range(8)] + [rk],
                                     first=False, last=True)
                    return fn
                postnorm([g], lambda d: V("g_mix_post", d), producer, bank_of=lambda g, d: obank[d])
            if len(segs) > 1:
                def producer(d):
                    slab, rk = load_slab(wout_d[d], 2048)
                    sv = slab[:, :2048].rearrange("p (k f) -> p k f", k=16)

                    def fn(g, b):
                        N = g.N
                        mm_group(b, N, [(sv[:, k, :], g.big[:, k, :N]) for k in range(16)], reads=[R(g, "big", k) for k in range(16)] + [rk])
                    return fn
                postnorm(segs[1:], lambda d: V("g_mix_post", d), producer)

        def attention(segs):
            for g in segs:
                prenorm(g, "g_mem_pre")
            for o in range(8):
                slab, rk = load_slab(wq_d[o], 1024)
                sv = slab[:, :1024].rearrange("p (k f) -> p k f", k=8)
                for g in segs:
                    N = g.N
                    b = newbank()
                    mm_group(b, N, [(sv[:, k, :], g.hT[:, k, :N]) for k in range(8)], reads=[R(g, "h", d) for d in range(8)] + [rk])
                    S.op("act", lambda e, o=o, b=b, g=g, N=N: e.activation(out=g.big[:, o, :N], in_=ps[b][:, :N], func=AF.Copy),
                         reads=[("ps", b)], writes=[R(g, "big", o)])
            for g in segs:
                N = g.N
                rinv = g.T[4]
                if not g.sample:
                    def s_mm(h):
                        bs2 = []
                        for mc in range(2):
                            b = newbank()
                            mm_group(b, N, [(kT_p[:, 2 * h + dc, mc * 128:(mc + 1) * 128], g.big[:, 2 * h + dc, :N]) for dc in range(2)],
                                     reads=[R(g, "big", 2 * h), R(g, "big", 2 * h + 1), "kT_p"])
                            bs2.append(b)
                        return bs2

                    def s_exp(h, banks):
                        pt = PT[h % 2]
                        for mc in range(2):
                            b = banks[mc]
                            S.op("act", lambda e, b=b, pt=pt, mc=mc, N=N: e.activation(out=pt[:, mc, :N], in_=ps[b][:, :N], func=AF.Exp, scale=1.0 / 16.0),
                                 reads=[("ps", b)], writes=[("PT", h % 2)])

                    sbanks = {0: s_mm(0)}
                    s_exp(0, sbanks[0])
                    sbanks[1] = s_mm(1)
                    for h in range(4):
                        pt = PT[h % 2]
                        bs = newbank()
                        mm_group(bs, N, [(ones[:, :], pt[:, mc, :N]) for mc in range(2)], reads=[("PT", h % 2), "ones"])
                        pv = []
                        for dch in range(2):
                            b = newbank()
                            mm_group(b, N, [(v_p[:, mc, h * 256 + dch * 128: h * 256 + dch * 128 + 128], pt[:, mc, :N]) for mc in range(2)],
                                     reads=[("PT", h % 2), "v_p"])
                            pv.append(b)
                        S.op("act", lambda e, bs=bs, N=N, rinv=rinv: e.activation(out=rinv[:, :N], in_=ps[bs][:, :N], func=AF.Ln),
                             reads=[("ps", bs)], writes=[R(g, "T", 4)])
                        S.op("act", lambda e, N=N, rinv=rinv: e.activation(out=rinv[:, :N], in_=rinv[:, :N], func=AF.Exp, scale=-1.0),
                             reads=[R(g, "T", 4)], writes=[R(g, "T", 4)])
                        for dch in range(2):
                            b = pv[dch]
                            ch = 8 + 2 * h + dch
                            S.op("dve", lambda e, b=b, ch=ch, g=g, N=N, rinv=rinv: e.tensor_tensor(out=g.big[:, ch, :N], in0=ps[b][:, :N], in1=rinv[:, :N], op=ALU.mult),
                                 reads=[("ps", b), R(g, "T", 4)], writes=[R(g, "big", ch)])
                        if h + 1 < 4:
                            s_exp(h + 1, sbanks[h + 1])
                        if h + 2 < 4:
                            sbanks[h + 2] = s_mm(h + 2)
                else:
                    state["pool"] = [0, 1, 2]
                    BO0, BO1, BSUM = 5, 6, 7
                    BSCS = [3, 4]
                    def sc_stage(r):
                        kslab, kk = load_slab(kTs_d[r], 2048, scr=kTs_s[r], skey=("k", r))
                        kv = kslab[:, :2048].rearrange("p (c m) -> p c m", c=8)
                        BSC = BSCS[r % 2]

                        def fsc(pe, kv=kv, r=r, BSC=BSC, g=g):
                            ins = None
                            for h in range(4):
                                for mc in range(2):
                                    for dc in range(2):
                                        col = (h * 2 + mc) * 8
                                        ins = pe.matmul(ps[BSC][:, col:col + 8], kv[:, 2 * h + dc, mc * 128:(mc + 1) * 128],
                                                        g.big[:, 2 * h + dc, r * 8:(r + 1) * 8], start=(dc == 0), stop=(dc == 1))
                            return ins
                        S.op("pe", fsc, reads=[R(g, "big", o) for o in range(8)] + [kk], writes=[("ps", BSC)])

                    sc_stage(0)
                    for r in range(NREQ):
                        vslab, vk = load_slab(vs_d[r], 2048, scr=vs_s[r], skey=("v", r))
                        vv = vslab[:, :2048].rearrange("p (c f) -> p c f", c=2)
                        sl = r % 2
                        BSC = BSCS[sl]
                        pts = PTs[sl]
                        S.op("act", lambda e, pts=pts, BSC=BSC: e.activation(out=pts[:, :], in_=ps[BSC][:, 0:64], func=AF.Exp, scale=1.0 / 16.0),
                             reads=[("ps", BSC)], writes=[("PTs", sl)])
                        if r + 1 < NREQ:
                            sc_stage(r + 1)

                        def fpv(pe, vv=vv, r=r, pts=pts):
                            ins = None
                            for h in range(4):
                                for dch in range(2):
                                    ch = 2 * h + dch
                                    bank = BO0 if ch < 4 else BO1
                                    col = (ch % 4) * 128 + r * 8
                                    for mc in range(2):
                                        ins = pe.matmul(ps[bank][:, col:col + 8], vv[:, mc, ch * 128:(ch + 1) * 128],
                                                        pts[:, (h * 2 + mc) * 8:(h * 2 + mc) * 8 + 8], start=(mc == 0), stop=(mc == 1))
                                col = h * 128 + r * 8
                                for mc in range(2):
                                    ins = pe.matmul(ps[BSUM][:, col:col + 8], ones[:, :], pts[:, (h * 2 + mc) * 8:(h * 2 + mc) * 8 + 8],
                                                    start=(mc == 0), stop=(mc == 1))
                            return ins
                        S.op("pe", fpv, reads=[("PTs", sl), vk, "ones"], writes=[("ps", BO0), ("ps", BO1), ("ps", BSUM)])
                    rv = P.sd
                    S.op("act", lambda e, rv=rv: e.activation(out=rv[:, :], in_=ps[BSUM][:, :], func=AF.Ln), reads=[("ps", BSUM)], writes=[R(P, "sd")])
                    S.op("act", lambda e, rv=rv: e.activation(out=rv[:, :], in_=rv[:, :], func=AF.Exp, scale=-1.0), reads=[R(P, "sd")], writes=[R(P, "sd")])
                    for ch in range(8):
                        bank = BO0 if ch < 4 else BO1
                        h = ch // 2
                        S.op("dve", lambda e, ch=ch, bank=bank, h=h, g=g, N=N, rv=rv: e.tensor_tensor(
                            out=g.big[:, 8 + ch, :N], in0=ps[bank][:, (ch % 4) * 128:(ch % 4) * 128 + 128], in1=rv[:, h * 128:(h + 1) * 128], op=ALU.mult),
                            reads=[("ps", bank), R(P, "sd")], writes=[R(g, "big", 8 + ch)])

            def producer(d):
                slab, rk = load_slab(wo_d[d], 1024)
                sv = slab[:, :1024].rearrange("p (k f) -> p k f", k=8)

                def fn(g, b):
                    N = g.N
                    if d == 0:
                        mm_group(b, N, [(sv[:, k, :], g.big[:, 8 + k, :N]) for k in range(8)], reads=[rk], fine=[[R(g, "big", 8 + k)] for k in range(8)])
                    else:
                        mm_group(b, N, [(sv[:, k, :], g.big[:, 8 + k, :N]) for k in range(8)], reads=[R(g, "big", 8 + k) for k in range(8)] + [rk])
                return fn
            postnorm(segs, lambda d: V("g_mem_post", d), producer)
            state["pool"] = list(range(8))

        for si in range(4):
            state["si"] = si
            T0 = si * 512
            segs = [P] + ([Sg] if si == 3 else [])
            ffn(segs, 0)
            if si == 0:
                S.dma("pool", wlru[:].rearrange("p h s j -> p (h s j)"), wlru_d, key="wlru", writes=["wlru"])
                S.dma("pool", gsb[:].rearrange("p c r t -> p (c r t)"), gs_d, key="gsb", writes=["gsb"])
                mem_kv()
                gs_tail_bounce()
                S.dma("sp", Sg.x[:, :, :], x_in[:, :, SEQ:SEQ + NS], key="xin_s", writes=XS_ALL)
                prenorm(Sg, "g_ff1_pre")
                state["early_pn_s"] = True
            mixer(segs)
            attention(segs)
            if si + 1 < 4:
                S.dma("sp", P.TX[:, :, :], x_in[:, :, T0 + 512:T0 + 1024], key="xin_p", writes=TX_ALL)
            def store_chunk(g, d, T0=T0):
                if g.sample:
                    S.dma("sp", y_out[:, d, SEQ:SEQ + NS], g.x[:, d, :], key=("yout_s", d), reads=[R(g, "x", d)])
                else:
                    S.dma("sp", y_out[:, d, T0:T0 + 512], g.x[:, d, :], key=("yout_p", d), reads=[R(g, "x", d)])
            if si + 1 < 4:
                p_sq, p_rest = early_prenorm_parts(P, "g_ff1_pre")
                ffn(segs, 1, on_x_done=store_chunk, mid_hook=p_sq, after_d=lambda d, p_rest=p_rest: (p_rest() if d == 1 else None))
                state["early_pn"] = True
            else:
                ffn(segs, 1, on_x_done=store_chunk)
        S.dma("sp", o_lc_p, lc32[:].rearrange("p c t -> p (c t)"), key="o_lc_p", reads=["lc32"])
        S.dma("sp", o_lh_p, h_carry[:, :], key="o_lh_p", reads=["h_carry"])
        S.dma("sp", o_cc_p, cc32[:].rearrange("p c t -> p (c t)"), key="o_cc_p", reads=["cc32"])
        S.dma("sp", o_lc_s, lcs32[:].rearrange("p c r t -> p (c r t)"), key="o_lc_s", reads=["lcs32"])
        S.dma("sp", o_lh_s, lhs32[:].rearrange("p c r -> p (c r)"), key="o_lh_s", reads=["lhs32"])
        S.dma("sp", o_cc_s_new, glus32[:].rearrange("p c r t -> p (c r t)"), key="o_cc_s", reads=["glus32"])
        S.final_wait_all("sp")
        S.emit()
    return nc


def _fm(v):
    return np.ascontiguousarray(v.reshape(8, 128).T)


def _prep_shared(inp):
    f = np.float32
    sh = {}
    vec_list = [inp[n][0] for n in VEC_NAMES] + [inp["lru_conv_w"][0][k] for k in range(4)] + [inp["conf_conv_w"][0][k] for k in range(31)]
    vecs = np.stack([_fm(np.asarray(v, f)) for v in vec_list], axis=1)
    sh["vecs"] = np.ascontiguousarray(vecs.reshape(128, NV * 8))
    sh["ident"] = np.eye(128, dtype=f)

    def gu(wg, wu):
        w = np.stack([wg, wu], 0).reshape(2, 8, 128, NFC, 128)
        return np.ascontiguousarray(w.transpose(3, 2, 0, 1, 4).reshape(NFC, 128, 2048))
    sh["wgu1"] = gu(inp["ff1_w_gate"][0], inp["ff1_w_up"][0])
    sh["wgu2"] = gu(inp["ff2_w_gate"][0], inp["ff2_w_up"][0])

    def dn(wd):
        w = wd.reshape(NFC, 128, 8, 128)
        return np.ascontiguousarray(w.transpose(2, 1, 0, 3).reshape(8, 128, NFC * 128))
    sh["wd1"] = dn(inp["ff1_w_down"][0])
    sh["wd2"] = dn(inp["ff2_w_down"][0])
    w = inp["w_in"][0].reshape(8, 128, 4, 8, 128)
    sh["win"] = np.ascontiguousarray(w.transpose(3, 1, 2, 0, 4).reshape(8, 128, 4096))
    wl = np.stack([inp["lru_w_a"][0], inp["lru_w_x"][0]], 0)
    sh["wlru"] = np.ascontiguousarray(wl.transpose(2, 1, 0, 3).reshape(128, 2048))
    w = inp["w_out"][0].reshape(16, 128, 8, 128)
    sh["wout"] = np.ascontiguousarray(w.transpose(2, 1, 0, 3).reshape(8, 128, 2048))
    for nm, key in (("wq", "w_q"), ("wo", "w_o"), ("wk", "w_mem_k")):
        w = inp[key][0].reshape(8, 128, 8, 128)
        sh[nm] = np.ascontiguousarray(w.transpose(2, 1, 0, 3).reshape(8, 128, 1024))
    w = inp["w_mem_v"][0].reshape(8, 128, 2, 512)
    sh["wv"] = np.ascontiguousarray(w.transpose(2, 1, 0, 3).reshape(2, 128, 4096))
    return sh


def _prep_core(inp, b):
    f = np.float32
    m = {}
    xp = inp["x_prompt"][b]
    xs = inp["x_sample"][16 * b:16 * b + 16].reshape(NS, D)
    xa = np.concatenate([xp, xs], 0)
    m["x_in"] = np.ascontiguousarray(xa.T.reshape(8, 128, NTOK).transpose(1, 0, 2))
    m["memT"] = np.ascontiguousarray(inp["mem_prompt"][b].T.reshape(8, 128, NMEM).transpose(1, 0, 2).reshape(128, 8 * NMEM))
    zs = inp["state_lru_conv"][0, 16 * b:16 * b + 16]
    m["zs_state"] = np.ascontiguousarray(zs.reshape(16, 3, 8, 128).transpose(3, 2, 0, 1).reshape(128, 8 * 16 * 3))
    h0 = inp["state_lru_h"][0, 16 * b:16 * b + 16]
    m["h0_state"] = np.ascontiguousarray(h0.reshape(16, 8, 128).transpose(2, 1, 0).reshape(128, 8 * 16))
    gs = inp["state_conf_conv"][0, 16 * b:16 * b + 16]
    gsl = gs.reshape(16, 30, 8, 128).transpose(3, 2, 0, 1)
    m["gs_state"] = np.ascontiguousarray(gsl.reshape(128, 8 * 16 * 30))
    m["gs_tail"] = np.ascontiguousarray(gsl[:, :, :, 8:30].reshape(128, 8 * 16 * 22))
    k = inp["cache_mem_k"][0, 16 * b:16 * b + 16]
    k = k.reshape(16, NMEM, 4, 2, 128)
    m["kTs"] = np.ascontiguousarray(k.transpose(0, 4, 2, 3, 1).reshape(16, 128, 2048))
    v = inp["cache_mem_v"][0, 16 * b:16 * b + 16].reshape(16, 2, 128, D)
    m["vs"] = np.ascontiguousarray(v.transpose(0, 2, 1, 3).reshape(16, 128, 2048))
    return m


_NC_CACHE = {}


def kernel(**inputs):
    inp = {k: np.asarray(v) for k, v in inputs.items()}
    if "nc" not in _NC_CACHE:
        _NC_CACHE["nc"] = build_program()
    nc = _NC_CACHE["nc"]
    shared = _prep_shared(inp)
    in_maps = []
    for b in range(NCORES):
        m = dict(shared)
        m.update(_prep_core(inp, b))
        in_maps.append(m)
    res = run_bass_kernel_spmd(nc, in_maps, core_ids=list(range(NCORES)))
    R = res.results
    f = np.float32

    def unfm(a):
        return a.transpose(2, 1, 0).reshape(a.shape[2], D)

    y_p = np.stack([unfm(R[b]["y_out"][:, :, :SEQ]) for b in range(NCORES)], 0)
    y_s = np.concatenate([unfm(R[b]["y_out"][:, :, SEQ:]).reshape(16, 8, D) for b in range(NCORES)], 0)
    lc_p = np.stack([unfm(R[b]["o_lc_p"].reshape(128, 8, 3)) for b in range(NCORES)], 0)[None]
    lh_p = np.stack([R[b]["o_lh_p"].T.reshape(D) for b in range(NCORES)], 0)[None]
    cc_p = np.stack([unfm(R[b]["o_cc_p"].reshape(128, 8, 30)) for b in range(NCORES)], 0)[None]
    k_p = np.stack([unfm(R[b]["o_k_p"].reshape(128, 8, NMEM)).reshape(NMEM, 4, 256) for b in range(NCORES)], 0)[None]
    v_p = np.stack([R[b]["o_v_p"].reshape(128, 2, D).transpose(1, 0, 2).reshape(NMEM, 4, 256) for b in range(NCORES)], 0)[None]

    def un_s(a, T):
        return a.transpose(2, 3, 1, 0).reshape(16, T, D)
    lc_s = np.concatenate([un_s(R[b]["o_lc_s"].reshape(128, 8, 16, 3), 3) for b in range(NCORES)], 0)[None]
    lh_s = np.concatenate([R[b]["o_lh_s"].reshape(128, 8, 16).transpose(2, 1, 0).reshape(16, D) for b in range(NCORES)], 0)[None]
    cc_s = np.concatenate([un_s(np.concatenate([R[b]["o_cc_s_old"].reshape(128, 8, 16, 22), R[b]["o_cc_s_new"].reshape(128, 8, 16, 8)], axis=3), 30) for b in range(NCORES)], 0)[None]
    outs = (y_p, y_s, lc_p, lh_p, cc_p, k_p, v_p, lc_s, lh_s, cc_s)
    return tuple(np.ascontiguousarray(o, dtype=f) for o in outs)
```
